# Optimizing a Trainium2 kernel written in Bass

```python
import math
import jax
import jax.numpy as jnp
from jax import lax
import numpy as np

D_MODEL = 1024
BATCH = 4
SEQ = 8192
DEPTH = 2

GRID_W = 64
CTX_LEN = 256
N_BRANCH = 3
N_MOD = 6
ATTN_HEADS = D_MODEL // 256
ATTN_HD = 64
ATTN_VD = 2 * ATTN_HD
ATTN_W = ATTN_HEADS * ATTN_VD
QBLOCK = 128
ROPE_BASE = 10000.0
ROPE_AXIS_DIM = ATTN_HD // 2
ROPE_FREQS = ROPE_AXIS_DIM // 2
LAMBDA_INIT_BASE = 0.8
LAMBDA_INIT_AMP = 0.6
LAMBDA_INIT_RATE = 0.3
POOL_WINDOWS = (2, 4, 8, 16)
POOL_GROUPS = len(POOL_WINDOWS)
POOL_W = D_MODEL // 2
POOL_GC = POOL_W // POOL_GROUPS
HYENA_W = D_MODEL // 2
HYENA_ORDER = 2
HYENA_SHORT = 3
HYENA_BANDS = 16
HYENA_EMB = 2 * HYENA_BANDS + 1
HYENA_FFN = 64
HYENA_FAST_DECAY = 0.3
HYENA_SLOW_DECAY = 1.5
HYENA_TARGET = 1e-2
FFN_HIDDEN = -(-(8 * D_MODEL) // (3 * 256)) * 256
Q0 = 0
K0 = Q0 + ATTN_W
V0 = K0 + ATTN_W
P0 = V0 + ATTN_W
H0 = P0 + POOL_W
G0 = H0 + (HYENA_ORDER + 1) * HYENA_W
IN_W = G0 + N_BRANCH * D_MODEL
EPS = 1e-6

kernel_name = 'hybrid_diffattn_pool_hyena_dit'


def rms_norm(x, g):
    xf = x.astype(jnp.float32)
    y = xf * lax.rsqrt(jnp.mean(xf * xf, axis=-1, keepdims=True) + EPS)
    return (y * g.astype(jnp.float32)).astype(x.dtype)


def modulate(h, shift, scale):
    return h * (1.0 + scale) + shift


def axial_rope_tables(rows):
    r = jnp.arange(rows, dtype=jnp.float32)
    cidx = jnp.arange(GRID_W, dtype=jnp.float32)
    row = jnp.broadcast_to(r[:, None], (rows, GRID_W)).reshape(-1)
    col = jnp.broadcast_to(cidx[None, :], (rows, GRID_W)).reshape(-1)
    inv = 1.0 / (ROPE_BASE ** (jnp.arange(ROPE_FREQS, dtype=jnp.float32) * 2.0 / ROPE_AXIS_DIM))
    ang = jnp.stack([row[:, None] * inv, col[:, None] * inv], axis=1)
    return jnp.cos(ang), jnp.sin(ang)


def apply_axial_rope(x, cos, sin):
    xr = x.astype(jnp.float32).reshape(x.shape[:-1] + (2, 2, ROPE_FREQS))
    c = cos[None, :, None, None]
    s = sin[None, :, None, None]
    a, b = xr[..., 0, :], xr[..., 1, :]
    out = jnp.stack([a * c - b * s, b * c + a * s], axis=-2)
    return out.reshape(x.shape).astype(x.dtype)


def qk_heads(t):
    return t.reshape(t.shape[:2] + (ATTN_HEADS, 2, ATTN_HD))


def v_heads(t):
    return t.reshape(t.shape[:2] + (ATTN_HEADS, ATTN_VD))


def diff_attend(q, k, v, lam):
    s = jnp.einsum('bqhmd,bkhmd->bhmqk', q, k, preferred_element_type=jnp.float32) * (ATTN_HD ** -0.5)
    p = jax.nn.softmax(s, axis=-1)
    a = p[:, :, 0] - lam * p[:, :, 1]
    return jnp.einsum('bhqk,bkhe->bqhe', a.astype(v.dtype), v)


def blocked_diff_attend(q, k, v, lam):
    b, l = q.shape[:2]
    nb = l // QBLOCK
    qb = jnp.moveaxis(q.reshape((b, nb, QBLOCK) + q.shape[2:]), 1, 0)
    ob = lax.map(lambda qq: diff_attend(qq, k, v, lam), qb)
    return jnp.moveaxis(ob, 0, 1).reshape((b, l) + ob.shape[3:])


def diff_attn_branch(o, lam_init, subln_g, w_o):
    o = rms_norm(o, subln_g) * (1.0 - lam_init)
    return o.reshape(o.shape[:2] + (ATTN_W,)) @ w_o


def pool_branch(u, w_pool, pool_scale, w_o):
    b, l, _ = u.shape
    uf = u.astype(jnp.float32)
    cs = jnp.concatenate([jnp.zeros((b, 1, POOL_W), jnp.float32), jnp.cumsum(uf, axis=1)], axis=1)
    t = jnp.arange(l)
    outs = []
    for g, w in enumerate(POOL_WINDOWS):
        hi = jnp.minimum(t + w // 2, l)
        lo = jnp.maximum(t - w // 2, 0)
        csg = cs[..., g * POOL_GC:(g + 1) * POOL_GC]
        mean = (csg[:, hi] - csg[:, lo]) / (hi - lo).astype(jnp.float32)[None, :, None]
        outs.append(mean - uf[..., g * POOL_GC:(g + 1) * POOL_GC])
    pooled = jnp.stack(outs, axis=2).astype(u.dtype)
    mixed = jnp.einsum('blgc,gcd->blgd', pooled, w_pool).reshape(b, l, POOL_W) * pool_scale
    return mixed @ w_o


def short_conv(u, w, bias):
    y = lax.conv_general_dilated(u, w[:, None, :].astype(u.dtype), window_strides=(1,),
                                 padding=[(HYENA_SHORT // 2, HYENA_SHORT // 2)],
                                 dimension_numbers=('NWC', 'WIO', 'NWC'),
                                 feature_group_count=u.shape[-1])
    return y + bias


def hyena_filters(l, w1, b1, f1, w2, b2, f2, w3):
    f32 = jnp.float32
    t01 = jnp.linspace(0.0, 1.0, l, dtype=f32)
    tr = jnp.arange(l, dtype=f32)
    bands = jnp.linspace(1e-4, HYENA_BANDS - 1, HYENA_BANDS, dtype=f32)
    ang = 2.0 * math.pi * tr[:, None] * bands[None, :] / l
    z = jnp.concatenate([t01[:, None], jnp.cos(ang), jnp.sin(ang)], axis=-1)
    h = jnp.sin(f1.astype(f32) * (z @ w1.astype(f32) + b1.astype(f32)))
    h = jnp.sin(f2.astype(f32) * (h @ w2.astype(f32) + b2.astype(f32)))
    h = (h @ w3.astype(f32)).reshape(l, 2, HYENA_ORDER, HYENA_W)
    max_decay = math.log(HYENA_TARGET) / HYENA_FAST_DECAY
    min_decay = math.log(HYENA_TARGET) / HYENA_SLOW_DECAY
    deltas = jnp.abs(jnp.linspace(min_decay, max_decay, HYENA_W, dtype=f32))
    h = h * jnp.exp(-t01[:, None] * deltas[None, :])[:, None, None, :]
    k = jnp.concatenate([h[:, 0], jnp.zeros((1, HYENA_ORDER, HYENA_W), f32), h[1:, 1][::-1]], axis=0)
    k = k / jnp.sum(jnp.abs(k), axis=0, keepdims=True)
    return jnp.fft.rfft(k, axis=0)


def fft_long_conv(u, kf, bias):
    l = u.shape[1]
    uf = u.astype(jnp.float32)
    y = jnp.fft.irfft(jnp.fft.rfft(uf, n=2 * l, axis=1) * kf[None], n=2 * l, axis=1)[:, :l]
    return (y + uf * bias.astype(jnp.float32)).astype(u.dtype)


def hyena_branch(u, w_short, b_short, w1, b1, f1, w2, b2, f2, w3, hy_bias, w_o):
    l = u.shape[1]
    u = short_conv(u, w_short, b_short)
    v, x1, x2 = jnp.split(u, HYENA_ORDER + 1, axis=-1)
    kf = hyena_filters(l, w1, b1, f1, w2, b2, f2, w3)
    z = x1 * fft_long_conv(v, kf[:, 0], hy_bias[0])
    y = x2 * fft_long_conv(z, kf[:, 1], hy_bias[1])
    return y @ w_o


def mix_stream(proj, attn_o, lam_init, subln_g, w_attn_o, w_pool, pool_scale, w_pool_o, w_short, b_short,
               hf_w1, hf_b1, hf_freq1, hf_w2, hf_b2, hf_freq2, hf_w3, hy_bias, w_hy_o, w_out):
    b, l, _ = proj.shape
    a = diff_attn_branch(attn_o, lam_init, subln_g, w_attn_o)
    p = pool_branch(proj[..., P0:H0], w_pool, pool_scale, w_pool_o)
    y = hyena_branch(proj[..., H0:G0], w_short, b_short, hf_w1, hf_b1, hf_freq1, hf_w2, hf_b2, hf_freq2,
                     hf_w3, hy_bias, w_hy_o)
    g = jax.nn.sigmoid(proj[..., G0:].reshape(b, l, N_BRANCH, D_MODEL))
    merged = g[:, :, 0] * a + g[:, :, 1] * p + g[:, :, 2] * y
    return merged @ w_out


def swiglu(h, w_in, w_out):
    gate, up = jnp.split(h @ w_in, 2, axis=-1)
    return (jax.nn.silu(gate) * up) @ w_out


def setup_inputs(seed: int = 0) -> dict:
    key = jax.random.key(seed)
    ks = iter(jax.random.split(key, 40))
    f32 = jnp.float32

    def nrm(shape, scale):
        return jax.random.normal(next(ks), shape, f32) * scale

    d = D_MODEL
    return {
        'x': nrm((BATCH, SEQ, d), 1.0),
        'c': nrm((BATCH, d), 1.0),
        'ctx': nrm((BATCH, CTX_LEN, d), 1.0),
        'c_ctx': nrm((d,), 1.0),
        'w_mod': nrm((DEPTH, d, N_MOD * d), 0.5 * d ** -0.5),
        'b_mod': nrm((DEPTH, N_MOD * d), 0.02),
        'norm1_g': 1.0 + nrm((DEPTH, d), 0.02),
        'norm2_g': 1.0 + nrm((DEPTH, d), 0.02),
        'w_in': nrm((DEPTH, d, IN_W), d ** -0.5),
        'lam_qk': nrm((DEPTH, 4, ATTN_HD), 0.1),
        'subln_g': 1.0 + nrm((DEPTH, ATTN_VD), 0.02),
        'w_attn_o': nrm((DEPTH, ATTN_W, d), ATTN_W ** -0.5),
        'w_pool': nrm((DEPTH, POOL_GROUPS, POOL_GC, POOL_GC), POOL_GC ** -0.5),
        'pool_scale': 1.0 + nrm((DEPTH, POOL_W), 0.02),
        'w_pool_o': nrm((DEPTH, POOL_W, d), POOL_W ** -0.5),
        'w_short': nrm((DEPTH, HYENA_SHORT, (HYENA_ORDER + 1) * HYENA_W), HYENA_SHORT ** -0.5),
        'b_short': nrm((DEPTH, (HYENA_ORDER + 1) * HYENA_W), 0.02),
        'hf_w1': nrm((DEPTH, HYENA_EMB, HYENA_FFN), HYENA_EMB ** -0.5),
        'hf_b1': nrm((DEPTH, HYENA_FFN), 0.02),
        'hf_freq1': 1.0 + nrm((DEPTH, HYENA_FFN), 0.02),
        'hf_w2': nrm((DEPTH, HYENA_FFN, HYENA_FFN), HYENA_FFN ** -0.5),
        'hf_b2': nrm((DEPTH, HYENA_FFN), 0.02),
        'hf_freq2': 1.0 + nrm((DEPTH, HYENA_FFN), 0.02),
        'hf_w3': nrm((DEPTH, HYENA_FFN, 2 * HYENA_ORDER * HYENA_W), HYENA_FFN ** -0.5),
        'hy_bias': nrm((DEPTH, HYENA_ORDER, HYENA_W), 0.5),
        'w_hy_o': nrm((DEPTH, HYENA_W, d), HYENA_W ** -0.5),
        'w_out': nrm((DEPTH, d, d), d ** -0.5),
        'w_ffn_in': nrm((DEPTH, d, 2 * FFN_HIDDEN), d ** -0.5),
        'w_ffn_out': nrm((DEPTH, FFN_HIDDEN, d), FFN_HIDDEN ** -0.5),
        'final_g': 1.0 + nrm((d,), 0.02),
    }


def reference(x, c, ctx, c_ctx, w_mod, b_mod, norm1_g, norm2_g, w_in, lam_qk, subln_g, w_attn_o,
              w_pool, pool_scale, w_pool_o, w_short, b_short, hf_w1, hf_b1, hf_freq1, hf_w2, hf_b2,
              hf_freq2, hf_w3, hy_bias, w_hy_o, w_out, w_ffn_in, w_ffn_out, final_g):
    rows = x.shape[1] // GRID_W
    cos, sin = axial_rope_tables(rows)
    c_act = jax.nn.silu(c)
    cc_act = jax.nn.silu(c_ctx)
    h_lat, h_ctx = x, ctx
    for l in range(DEPTH):
        last = l == DEPTH - 1
        lam_init = LAMBDA_INIT_BASE - LAMBDA_INIT_AMP * math.exp(-LAMBDA_INIT_RATE * l)
        lq = lam_qk[l].astype(jnp.float32)
        lam = jnp.exp(jnp.sum(lq[0] * lq[1])) - jnp.exp(jnp.sum(lq[2] * lq[3])) + lam_init
        mod = jnp.split((c_act @ w_mod[l] + b_mod[l])[:, None, :], N_MOD, axis=-1)
        mod_c = jnp.split((cc_act @ w_mod[l] + b_mod[l])[None, None, :], N_MOD, axis=-1)
        mix_params = (subln_g[l], w_attn_o[l], w_pool[l], pool_scale[l], w_pool_o[l], w_short[l], b_short[l],
                      hf_w1[l], hf_b1[l], hf_freq1[l], hf_w2[l], hf_b2[l], hf_freq2[l], hf_w3[l], hy_bias[l],
                      w_hy_o[l], w_out[l])

        a_lat = modulate(rms_norm(h_lat, norm1_g[l]), mod[0], mod[1])
        a_ctx = modulate(rms_norm(h_ctx, norm1_g[l]), mod_c[0], mod_c[1])
        proj = a_lat @ w_in[l]
        q = apply_axial_rope(qk_heads(proj[..., Q0:K0]), cos, sin)
        k = apply_axial_rope(qk_heads(proj[..., K0:V0]), cos, sin)
        v = v_heads(proj[..., V0:P0])
        if last:
            proj_kv = a_ctx @ w_in[l][:, K0:P0]
            k_c = qk_heads(proj_kv[..., :ATTN_W])
            v_c = v_heads(proj_kv[..., ATTN_W:])
        else:
            proj_c = a_ctx @ w_in[l]
            k_c = qk_heads(proj_c[..., K0:V0])
            v_c = v_heads(proj_c[..., V0:P0])
            o_c = diff_attend(qk_heads(proj_c[..., Q0:K0]), k_c, v_c, lam)
            h_ctx_mid = h_ctx + mod_c[2] * mix_stream(proj_c, o_c, lam_init, *mix_params)
            h_ctx = h_ctx_mid + mod_c[5] * swiglu(
                modulate(rms_norm(h_ctx_mid, norm2_g[l]), mod_c[3], mod_c[4]), w_ffn_in[l], w_ffn_out[l])
        o = blocked_diff_attend(q, jnp.concatenate([k_c, k], axis=1), jnp.concatenate([v_c, v], axis=1), lam)
        h_lat = h_lat + mod[2] * mix_stream(proj, o, lam_init, *mix_params)

        h_lat = h_lat + mod[5] * swiglu(
            modulate(rms_norm(h_lat, norm2_g[l]), mod[3], mod[4]), w_ffn_in[l], w_ffn_out[l])
    return rms_norm(h_lat, final_g)
```

```python
import math
from contextlib import ExitStack
import numpy as np
import ml_dtypes
import concourse.bass as bass
import concourse.mybir as mybir
from concourse.bass_utils import run_bass_kernel_spmd

F32 = mybir.dt.float32
BF16 = mybir.dt.bfloat16
AF = mybir.ActivationFunctionType
ALU = mybir.AluOpType
AX = mybir.AxisListType
NPBF = ml_dtypes.bfloat16

D = 1024
INW = 6656
NMOD = 6
FFN = 2816
CTX = 256
EPS = 1e-6
SELF_SYNC = True


class Dep:
    __slots__ = ("name", "writers", "readers", "dsem", "dcnt")

    def __init__(self, name=""):
        self.name = name
        self.writers = {}
        self.readers = {}
        self.dsem = {}
        self.dcnt = {}


class FW:
    def __init__(self, nc):
        self.nc = nc
        self.E = {"pe": nc.tensor, "dve": nc.vector, "act": nc.scalar, "pool": nc.gpsimd, "sp": nc.sync}
        self.sems = {}
        self.cnt = {}
        self.semobj = {}
        for k in ("pe", "dve", "act", "pool"):
            self.sems[k] = nc.alloc_semaphore(name="c_" + k)
            self.cnt[k] = 0
            self.semobj[("c", k)] = self.sems[k]
        self.waited = {k: {} for k in self.E}
        self.ndsem = 0
        self.es = None
        self.uid = 0
        self.free_dsem = {}
        self.all_dma_deps = []
        self.n_inst = 0
        self.n_wait = 0

    def sb(self, name, shape, dt=F32):
        self.uid += 1
        name = "%s_u%d" % (name, self.uid)
        if self.es is not None:
            return self.es.enter_context(self.nc.sbuf_tensor(name, list(shape), dt))
        return self.nc.alloc_sbuf_tensor(name, list(shape), dt)

    def barrier(self):
        needs = {("c", k): v for k, v in self.cnt.items() if v > 0}
        for d, q in self.all_dma_deps:
            if d.dcnt[q] > 0:
                needs[d.dsem[q]] = d.dcnt[q]
        for eng in self.E:
            self._wait(eng, dict(needs))
        for d, q in self.all_dma_deps:
            self.free_dsem.setdefault(q, []).append((d.dsem[q], d.dcnt[q]))
            del d.dsem[q]
            del d.dcnt[q]
        self.all_dma_deps = []

    def ps(self, name, shape, dt=F32):
        return self.nc.alloc_psum_tensor(name, list(shape), dt)

    def _dsem(self, d, q):
        if q not in d.dsem:
            fp = self.free_dsem.setdefault(q, [])
            if fp:
                d.dsem[q], d.dcnt[q] = fp.pop()
            else:
                d.dsem[q] = ("d", self.ndsem)
                self.semobj[d.dsem[q]] = self.nc.alloc_semaphore(name="d_%d" % self.ndsem)
                self.ndsem += 1
                d.dcnt[q] = 0
            self.all_dma_deps.append((d, q))
        return d.dsem[q]

    def _wait(self, eng, needs):
        e = self.E[eng]
        w = self.waited[eng]
        for key, val in needs.items():
            if key == ("c", eng) and (eng == "pe" or not SELF_SYNC):
                continue
            if w.get(key, 0) >= val:
                continue
            e.wait_ge(self.semobj[key], val)
            w[key] = val
            self.n_wait += 1

    @staticmethod
    def _needs(reads, writes, accs):
        needs = {}
        for d in reads:
            for k, v in d.writers.items():
                if needs.get(k, 0) < v:
                    needs[k] = v
        for d in list(writes) + list(accs):
            for src in (d.writers, d.readers):
                for k, v in src.items():
                    if needs.get(k, 0) < v:
                        needs[k] = v
        return needs

    @staticmethod
    def _commit(tok, reads, writes, accs):
        k, v = tok
        for d in reads:
            if d.readers.get(k, 0) < v:
                d.readers[k] = v
        for d in writes:
            d.writers = {k: v}
            d.readers = {}
        for d in accs:
            if d.writers.get(k, 0) < v:
                d.writers[k] = v
            d.readers = {}

    def op(self, eng, emit, reads=(), writes=(), accs=()):
        self._wait(eng, self._needs(reads, writes, accs))
        ins = emit(self.E[eng])
        self.cnt[eng] += 1
        ins.then_inc(self.sems[eng], 1)
        self.n_inst += 1
        self._commit((("c", eng), self.cnt[eng]), reads, writes, accs)
        return ins

    def dma(self, q, out, in_, reads=(), writes=(), accs=(), track=None, **kw):
        self._wait(q, self._needs(reads, writes, accs))
        if track is None:
            track = (list(writes) + list(accs) + list(reads))[0]
        key = self._dsem(track, q)
        ins = self.E[q].dma_start(out=out, in_=in_, **kw)
        track.dcnt[q] += 16
        ins.then_inc(self.semobj[key], 16)
        self.n_inst += 1
        self._commit((key, track.dcnt[q]), reads, writes, accs)
        return ins

    def finish(self, deps, eng="sp"):
        needs = {}
        for d in deps:
            for src in (d.writers, d.readers):
                for k, v in src.items():
                    if needs.get(k, 0) < v:
                        needs[k] = v
        self._wait(eng, needs)


class Ring:
    def __init__(self, fw, name, n, shape, dt=F32, psum=False):
        self.t = []
        self.d = []
        for i in range(n):
            nm = "%s%d" % (name, i)
            self.t.append(fw.ps(nm, shape, dt) if psum else fw.sb(nm, shape, dt))
            self.d.append(Dep(nm))
        self.i = 0
        self.n = n

    def next(self):
        i = self.i
        self.i = (i + 1) % self.n
        return self.t[i], self.d[i]


def sbap(t, rowlen, offset, dims, npart=128):
    return bass.AP(tensor=t, offset=offset, ap=[[rowlen, npart]] + [list(x) for x in dims])


def make_ident(fw, name="ident"):
    ident = fw.sb(name, [128, 128], F32)
    d = Dep(name)
    fw.op("pool", lambda e: e.memset(ident[:], 0.0), writes=[d])
    fw.op("pool", lambda e: e.affine_select(out=ident[:], in_=ident[:], pattern=[[-1, 128]],
                                            compare_op=ALU.not_equal, fill=1.0, base=0, channel_multiplier=1),
          reads=[d], writes=[d])
    return ident, d


def emit_mod(fw, cvec, w_mod, b_mod, cols, pj, out_tiles, ident_ones, ident_t):
    nc = fw.nc
    ones, d_ones = ident_ones
    ident, d_id = ident_t
    cR = fw.sb("cR", [16, 128], F32)
    d_cR = Dep("cR")
    fw.dma("sp", cR[:], cvec.ap().rearrange("r (k p) -> (r k) p", p=128), writes=[d_cR])
    pt, d_pt = pj.next()
    fw.op("pe", lambda e: e.transpose(pt[:, 0:16], cR[:], ident[0:16, 0:16]), reads=[d_cR, d_id], writes=[d_pt])
    cA = fw.sb("cA", [128, 16], F32)
    d_cA = Dep("cA")
    fw.op("act", lambda e: e.activation(out=cA[:], in_=pt[:, 0:16], func=AF.Silu), reads=[d_pt], writes=[d_cA])
    rep = fw.sb("crep", [128, 16 * 128], F32)
    d_rep = Dep("crep")
    for i in range(16):
        fw.op("dve", lambda e, i=i: e.tensor_scalar(out=rep[:, i * 128:(i + 1) * 128], in0=ones[:, 0:128],
                                                     scalar1=cA[:, i:i + 1], scalar2=None, op0=ALU.mult),
              reads=[d_cA, d_ones], accs=[d_rep])
    c_lo = cols[0] * 512
    bm = fw.sb("bmod", [1, len(cols) * 512], F32)
    d_bm = Dep("bmod")
    fw.dma("sp", bm[:], b_mod[:, c_lo:c_lo + len(cols) * 512], writes=[d_bm])
    wst = Ring(fw, "wmst", 3, [128, 512], F32)
    for r in range(2):
        ot, d_ot = out_tiles[r]
        for j, nb in enumerate(cols):
            pt, d_pt = pj.next()
            for kc in range(8):
                wt, d_wt = wst.next()
                fw.dma("sp", wt[:], w_mod[kc * 128:(kc + 1) * 128, nb * 512:(nb + 1) * 512], writes=[d_wt])
                fw.op("pe", lambda e, kc=kc, wt=wt, pt=pt, r=r: e.matmul(
                    pt[:], lhsT=rep[:, (r * 8 + kc) * 128:(r * 8 + kc + 1) * 128], rhs=wt[:],
                    start=(kc == 0), stop=False), reads=[d_rep, d_wt], accs=[d_pt])
            fw.op("pe", lambda e, pt=pt, nb=nb: e.matmul(pt[:], lhsT=ones[0:1, 0:128], rhs=bm[0:1, nb * 512 - c_lo:(nb + 1) * 512 - c_lo],
                                                        start=False, stop=True), reads=[d_ones, d_bm], accs=[d_pt])
            fw.op("act", lambda e, pt=pt, ot=ot, j=j: e.copy(out=ot[:, j * 512:(j + 1) * 512], in_=pt[:]),
                  reads=[d_pt], accs=[d_ot])


def load_rep(fw, name, src_row_ap, n, q="sp"):
    t = fw.sb(name, [128, n], F32)
    d = Dep(name)
    fw.dma(q, t[:], src_row_ap.partition_broadcast(128), writes=[d])
    return t, d


def load_w_bf16(fw, wb, d_wb, w, K, N, stage_ring, chunk=2048):
    kc_n = K // 128
    i = 0
    for kc in range(kc_n):
        for c0 in range(0, N, chunk):
            c1 = min(N, c0 + chunk)
            st, d_st = stage_ring.next()
            fw.dma("sp", st[:, 0:c1 - c0], w[kc * 128:(kc + 1) * 128, c0:c1], writes=[d_st])
            eng = "act" if (i % 2 == 0) else "dve"
            if eng == "act":
                fw.op("act", lambda e, st=st, kc=kc, c0=c0, c1=c1: e.copy(out=wb[:, kc * N + c0:kc * N + c1], in_=st[:, 0:c1 - c0]),
                      reads=[d_st], accs=[d_wb])
            else:
                fw.op("dve", lambda e, st=st, kc=kc, c0=c0, c1=c1: e.tensor_copy(out=wb[:, kc * N + c0:kc * N + c1], in_=st[:, 0:c1 - c0]),
                      reads=[d_st], accs=[d_wb])
            i += 1
    return


H0C = 2048
G0C = 3584
LAM_INIT = [0.8 - 0.6 * math.exp(-0.3 * l) for l in range(2)]
MAGIC = 12582912.0
POOL_WIN = (2, 4, 8, 16)


class P:
    pass


def cp(fw, eng, out, in_, reads, writes=(), accs=()):
    if eng == "act":
        return fw.op("act", lambda e: e.copy(out=out, in_=in_), reads=reads, writes=writes, accs=accs)
    return fw.op(eng, lambda e: e.tensor_copy(out=out, in_=in_), reads=reads, writes=writes, accs=accs)


def phase_begin(fw):
    es = ExitStack()
    fw.es = es
    return es


def phase_end(fw, es):
    fw.barrier()
    fw.es = None
    es.close()


def emit_A(p, l):
    fw, nc = p.fw, p.nc
    W = p.W[l]
    es = phase_begin(fw)
    pj = p.pj
    tp = p.tp
    ident, d_id = p.ident
    wb = fw.sb("w_in_b", [128, 8 * INW], BF16)
    d_wb = Dep("w_in_b")
    G = [(fw.sb("G%d" % r, [128, D], F32), Dep("G%d" % r)) for r in range(2)]
    SH = [(fw.sb("SH%d" % r, [128, D], F32), Dep("SH%d" % r)) for r in range(2)]
    es2 = ExitStack()
    fw.es = es2
    modt = [(fw.sb("mod%d" % r, [128, 2048], F32), Dep("mod%d" % r)) for r in range(2)]
    emit_mod(fw, p.cvec, W["w_mod"], W["b_mod"], [0, 1, 2, 3], pj, modt, p.ones, p.ident)
    g1t, d_g1 = load_rep(fw, "g1t", W["g1"].ap(), D)
    for r in range(2):
        gt, d_gt = G[r]
        sh, d_sh = SH[r]
        mt, d_mt = modt[r]
        fw.op("dve", lambda e: e.scalar_tensor_tensor(out=gt[:], in0=mt[:, 1024:2048], scalar=1.0, in1=g1t[:],
                                                      op0=ALU.add, op1=ALU.mult), reads=[d_mt, d_g1], writes=[d_gt])
        cp(fw, "dve", sh[:], mt[:, 0:1024], [d_mt], writes=[d_sh])
    stg = Ring(fw, "wstg", 2, [128, 2048], F32)
    load_w_bf16(fw, wb, d_wb, W["w_in"], D, INW, stg)
    fw.barrier()
    es2.close()
    fw.es = es
    hbuf = Ring(fw, "hbuf", 2, [128, D], F32)
    abuf = Ring(fw, "abuf", 1, [128, D], F32)
    aT = Ring(fw, "aT", 2, [128, 8 * 128], BF16)
    ybuf = Ring(fw, "ybuf", 2, [128, INW], BF16)
    yrb = Ring(fw, "yrb", 2, [128, 1024], BF16)
    qkf = Ring(fw, "qkf", 1, [128, 1024], F32)
    csb = Ring(fw, "csb", 2, [128, 64], F32)
    tmp = Ring(fw, "rtmp", 2, [128, 512], F32)
    sq = fw.sb("sq", [128, D], F32)
    d_sq = Dep("sq")
    stat = Ring(fw, "stat", 2, [128, 2], F32)
    Jb, d_J = p.Jb
    for t in range(p.NT):
        r = 1 if t >= p.NL else 0
        ht, d_ht = hbuf.next()
        fw.dma("sp", ht[:], p.h_src(l, t), writes=[d_ht])
        ct, d_ct = csb.next()
        fw.dma("sp", ct[:], p.cs[t * 128:(t + 1) * 128, :], writes=[d_ct])
        st, d_st = stat.next()
        fw.op("act", lambda e: e.activation(out=sq[:], in_=ht[:], func=AF.Square, accum_out=st[:, 0:1]),
              reads=[d_ht], writes=[d_sq, d_st])
        fw.op("dve", lambda e: e.tensor_scalar(out=st[:, 1:2], in0=st[:, 0:1], scalar1=1.0 / D, scalar2=EPS,
                                               op0=ALU.mult, op1=ALU.add), reads=[d_st], accs=[d_st])
        fw.op("act", lambda e: e.activation(out=st[:, 1:2], in_=st[:, 1:2], func=AF.Sqrt), reads=[d_st], accs=[d_st])
        fw.op("dve", lambda e: e.reciprocal(out=st[:, 1:2], in_=st[:, 1:2]), reads=[d_st], accs=[d_st])
        at, d_at = abuf.next()
        gt, d_gt = G[r]
        sh, d_sh = SH[r]
        fw.op("dve", lambda e: e.scalar_tensor_tensor(out=at[:], in0=ht[:], scalar=st[:, 1:2], in1=gt[:],
                                                      op0=ALU.mult, op1=ALU.mult), reads=[d_ht, d_st, d_gt], writes=[d_at])
        fw.op("dve", lambda e: e.tensor_add(out=at[:], in0=at[:], in1=sh[:]), reads=[d_at, d_sh], writes=[d_at])
        aTt, d_aT = aT.next()
        for half in range(2):
            pt, d_pt = tp.next()
            for j in range(4):
                kc = half * 4 + j
                fw.op("pe", lambda e: e.transpose(pt[:, j * 128:(j + 1) * 128], at[:, kc * 128:(kc + 1) * 128], ident[:]),
                      reads=[d_at, d_id], accs=[d_pt])
            cp(fw, "dve" if half == 0 else "act", aTt[:, half * 512:(half + 1) * 512], pt[:], [d_pt], accs=[d_aT])
        yt, d_yt = ybuf.next()
        qt, d_qt = qkf.next()
        for nb in range(13):
            pt, d_pt = pj.next()
            for kc in range(8):
                fw.op("pe", lambda e: e.matmul(pt[:], lhsT=aTt[:, kc * 128:(kc + 1) * 128],
                                               rhs=wb[:, kc * INW + nb * 512:kc * INW + (nb + 1) * 512],
                                               start=(kc == 0), stop=(kc == 7)), reads=[d_aT, d_wb], accs=[d_pt])
            if nb < 2:
                cp(fw, "act", qt[:, nb * 512:(nb + 1) * 512], pt[:], [d_pt], accs=[d_qt])
            elif nb < 7:
                cp(fw, "act", yt[:, nb * 512:(nb + 1) * 512], pt[:], [d_pt], accs=[d_yt])
            else:
                fw.op("act", lambda e: e.activation(out=yt[:, nb * 512:(nb + 1) * 512], in_=pt[:], func=AF.Sigmoid),
                      reads=[d_pt], accs=[d_yt])
            if nb == 1:
                t1, d_t1 = tmp.next()
                t2, d_t2 = tmp.next()
                xa = sbap(qt, 1024, 0, [[64, 16], [32, 2], [1, 16]])
                xb = sbap(qt, 1024, 16, [[64, 16], [32, 2], [1, 16]])
                cosv = sbap(ct, 64, 0, [[0, 16], [16, 2], [1, 16]])
                sinv = sbap(ct, 64, 32, [[0, 16], [16, 2], [1, 16]])
                oa = sbap(yt, INW, 0, [[64, 16], [32, 2], [1, 16]])
                ob = sbap(yt, INW, 16, [[64, 16], [32, 2], [1, 16]])
                v1 = sbap(t1, 512, 0, [[32, 16], [16, 2], [1, 16]])
                v2 = sbap(t2, 512, 0, [[32, 16], [16, 2], [1, 16]])
                fw.op("dve", lambda e: e.tensor_tensor(out=v1, in0=xa, in1=cosv, op=ALU.mult), reads=[d_qt, d_ct], writes=[d_t1])
                fw.op("dve", lambda e: e.tensor_tensor(out=v2, in0=xb, in1=sinv, op=ALU.mult), reads=[d_qt, d_ct], writes=[d_t2])
                fw.op("dve", lambda e: e.tensor_tensor(out=oa, in0=v1, in1=v2, op=ALU.subtract), reads=[d_t1, d_t2], accs=[d_yt])
                fw.op("dve", lambda e: e.tensor_tensor(out=v1, in0=xb, in1=cosv, op=ALU.mult), reads=[d_qt, d_ct], writes=[d_t1])
                fw.op("dve", lambda e: e.tensor_tensor(out=v2, in0=xa, in1=sinv, op=ALU.mult), reads=[d_qt, d_ct], writes=[d_t2])
                fw.op("dve", lambda e: e.tensor_tensor(out=ob, in0=v1, in1=v2, op=ALU.add), reads=[d_t1, d_t2], accs=[d_yt])
        fw.dma("pool", p.Y[t * 128:(t + 1) * 128, :], yt[:], reads=[d_yt], track=d_yt)
        yr, d_yr = yrb.next()
        for k, c0 in enumerate((1536, H0C + 512)):
            pt, d_pt = pj.next()
            fw.op("pe", lambda e: e.matmul(pt[:], lhsT=Jb[:], rhs=yt[:, c0:c0 + 512], start=True, stop=True),
                  reads=[d_J, d_yt], writes=[d_pt])
            cp(fw, "dve", yr[:, k * 512:(k + 1) * 512], pt[:], [d_pt], accs=[d_yr])
        fw.dma("pool", p.YR[t * 128:(t + 1) * 128, :], yr[:], reads=[d_yr], track=d_yr)
    phase_end(fw, es)


def sin_layer(fw, p, ps, nrows, bcol, fcol, out, d_out, ncol, rd):
    t1, d_t1 = p.sring.next()
    t2, d_t2 = p.sring.next()
    fw.op("dve", lambda e: e.tensor_scalar(out=t1[0:nrows, 0:ncol], in0=ps, scalar1=bcol, scalar2=fcol,
                                           op0=ALU.add, op1=ALU.mult), reads=rd, writes=[d_t1])
    fw.op("dve", lambda e: e.tensor_scalar(out=t2[0:nrows, 0:ncol], in0=t1[0:nrows, 0:ncol], scalar1=MAGIC, scalar2=MAGIC,
                                           op0=ALU.add, op1=ALU.subtract), reads=[d_t1], writes=[d_t2])
    fw.op("dve", lambda e: e.tensor_tensor(out=t1[0:nrows, 0:ncol], in0=t1[0:nrows, 0:ncol], in1=t2[0:nrows, 0:ncol],
                                           op=ALU.subtract), reads=[d_t1, d_t2], writes=[d_t1])
    fw.op("act", lambda e: e.activation(out=out, in_=t1[0:nrows, 0:ncol], func=AF.Sin, scale=2 * math.pi),
          reads=[d_t1], accs=[d_out])


def store_shifted(fw, p, src, d_src, G, nrows, row0, flen):
    w = flen + 128
    pitch = flen + 256
    for b in range(32):
        dst = bass.AP(tensor=G, offset=(row0 * 32 + b) * pitch, ap=[[32 * pitch, 128], [1, w]])
        if b == 0:
            fw.dma("sp", dst, src[:, 0:w], reads=[d_src], track=d_src)
            continue
        tt, d_tt = p.shring.next()
        eng = ("act", "dve", "pool")[b % 3]
        cp(fw, eng, tt[:, 0:w], src[:, b:b + w], [d_src], writes=[d_tt])
        fw.dma("sp", dst, tt[:, 0:w], reads=[d_tt], track=d_tt)


def emit_F(p, l, do_ctx):
    fw, nc = p.fw, p.nc
    W = p.W[l]
    es = phase_begin(fw)
    ident, d_id = p.ident
    ones, d_ones = p.ones
    pj = p.pj
    def ld(name, src, shape):
        t = fw.sb(name, shape, F32)
        d = Dep(name)
        fw.dma("sp", t[:], src, writes=[d])
        return t, d
    w1, d_w1 = ld("hw1", W["hf_w1"].ap(), [33, 64])
    w2, d_w2 = ld("hw2", W["hf_w2"].ap(), [64, 64])
    w3, d_w3 = ld("hw3", W["hf_w3"].ap(), [64, 2048])
    pr, d_pr = ld("hpr", W["hf_par"].ap(), [64, 4])
    nd, d_nd = ld("negd", p.negdelta.ap(), [128, 4])
    hyb, d_hyb = ld("hyb", W["hyb"].ap(), [128, 8])
    fdiv = fw.sb("fdiv", [64, 2], F32)
    d_fd = Dep("fdiv")
    fw.op("dve", lambda e: e.tensor_scalar(out=fdiv[:, 0:1], in0=pr[:, 1:2], scalar1=1.0 / (2 * math.pi), scalar2=None, op0=ALU.mult),
          reads=[d_pr], accs=[d_fd])
    fw.op("dve", lambda e: e.tensor_scalar(out=fdiv[:, 1:2], in0=pr[:, 3:4], scalar1=1.0 / (2 * math.pi), scalar2=None, op0=ALU.mult),
          reads=[d_pr], accs=[d_fd])
    p.sring = Ring(fw, "sring", 2, [128, 512], F32)
    wsr, d_wsr = ld("wsr", W["w_short"].ap(), [3, 1536])
    p.shring = Ring(fw, "shring", 2, [128, 2 * p.S + 128], BF16)
    fst = fw.sb("fst", [128, 512 + 160], BF16)
    d_fst = Dep("fst")
    for g in range(12):
        stream = g // 4
        pt, d_pt = pj.next()
        fw.op("pe", lambda e: e.transpose(pt[:, 0:3], wsr[0:3, g * 128:(g + 1) * 128], ident[0:3, 0:3]),
              reads=[d_wsr, d_id], writes=[d_pt])
        fw.op("dve", lambda e: e.memset(fst[:], 0.0), writes=[d_fst])
        for j in range(3):
            idx = (257 - j) if stream == 1 else (254 + j)
            cp(fw, "dve", fst[:, idx:idx + 1], pt[:, j:j + 1], [d_pt], accs=[d_fst])
        store_shifted(fw, p, fst, d_fst, p.FS, 1536, g * 128, 512)
    hd = [(fw.sb("hd%d" % i, [128, p.S], F32), Dep("hd%d" % i)) for i in range(2)]
    kf = fw.sb("kfrow", [128, 2 * p.S + 160], BF16)
    d_kf = Dep("kfrow")
    h1 = fw.sb("h1", [64, 512], F32)
    d_h1 = Dep("h1")
    h2 = fw.sb("h2", [64, 512], F32)
    d_h2 = Dep("h2")
    ztile = Ring(fw, "ztile", 2, [33, 512], F32)
    dec = Ring(fw, "dec", 2, [128, 512], F32)
    asum = fw.sb("asum", [128, 64], F32)
    d_as = Dep("asum")
    tot = fw.sb("tot", [128, 4], F32)
    d_tot = Dep("tot")
    seqs = [("lat", p.S, p.zt_lat, p.FL)]
    if do_ctx:
        seqs.append(("ctx", CTX, p.zt_ctx, p.FC))
    for (sname, L, zt, Fdst) in seqs:
        cb = min(512, L)
        ncb = L // cb
        for order in range(2):
            for cg in range(4):
                fw.op("dve", lambda e: e.memset(asum[:], 0.0), writes=[d_as])
                for dr in range(2):
                    rev = (dr == 1) if order == 0 else (dr == 0)
                    hdt, d_hd = hd[dr]
                    col0 = dr * 1024 + order * 512 + cg * 128
                    for b in range(ncb):
                        z, d_z = ztile.next()
                        fw.dma("sp", z[:, 0:cb], zt[1 if rev else 0, :, b * cb:(b + 1) * cb], writes=[d_z])
                        pa, d_pa = pj.next()
                        fw.op("pe", lambda e: e.matmul(pa[0:64, 0:cb], lhsT=w1[:, :], rhs=z[:, 0:cb], start=True, stop=True),
                              reads=[d_w1, d_z], writes=[d_pa])
                        sin_layer(fw, p, pa[0:64, 0:cb], 64, pr[:, 0:1], fdiv[:, 0:1], h1[:, 0:cb], d_h1, cb, [d_pa, d_pr, d_fd])
                        pb, d_pb = pj.next()
                        fw.op("pe", lambda e: e.matmul(pb[0:64, 0:cb], lhsT=w2[:, :], rhs=h1[:, 0:cb], start=True, stop=True),
                              reads=[d_w2, d_h1], writes=[d_pb])
                        sin_layer(fw, p, pb[0:64, 0:cb], 64, pr[:, 2:3], fdiv[:, 1:2], h2[:, 0:cb], d_h2, cb, [d_pb, d_pr, d_fd])
                        pc, d_pc = pj.next()
                        fw.op("pe", lambda e: e.matmul(pc[:, 0:cb], lhsT=w3[:, col0:col0 + 128], rhs=h2[:, 0:cb], start=True, stop=True),
                              reads=[d_w3, d_h2], writes=[d_pc])
                        pd, d_pd = pj.next()
                        fw.op("pe", lambda e: e.matmul(pd[:, 0:cb], lhsT=ones[0:1, 0:128], rhs=z[0:1, 0:cb], start=True, stop=True),
                              reads=[d_ones, d_z], writes=[d_pd])
                        dt_, d_dt = dec.next()
                        fw.op("act", lambda e: e.activation(out=dt_[:, 0:cb], in_=pd[:, 0:cb], func=AF.Exp, scale=nd[:, cg:cg + 1]),
                              reads=[d_pd, d_nd], writes=[d_dt])
                        fw.op("dve", lambda e: e.tensor_tensor(out=hdt[:, b * cb:(b + 1) * cb], in0=dt_[:, 0:cb], in1=pc[:, 0:cb], op=ALU.mult),
                              reads=[d_dt, d_pc], accs=[d_hd])
                        lo, hi = b * cb, (b + 1) * cb
                        if dr == 1:
                            if rev and b == ncb - 1:
                                hi -= 1
                            if (not rev) and b == 0:
                                lo += 1
                        fw.op("dve", lambda e: e.tensor_reduce(out=asum[:, dr * 32 + b:dr * 32 + b + 1], in_=hdt[:, lo:hi],
                                                               axis=AX.X, op=ALU.add, apply_absolute_value=True),
                              reads=[d_hd], accs=[d_as])
                fw.op("dve", lambda e: e.tensor_reduce(out=tot[:, 0:1], in_=asum[:], axis=AX.X, op=ALU.add), reads=[d_as], writes=[d_tot])
                fw.op("dve", lambda e: e.reciprocal(out=tot[:, 1:2], in_=tot[:, 0:1]), reads=[d_tot], accs=[d_tot])
                hf, d_hf = hd[0]
                hb_, d_hb = hd[1]
                bcol = hyb[:, order * 4 + cg:order * 4 + cg + 1]
                if order == 0:
                    fw.op("dve", lambda e: e.memset(kf[:, 0:1], 0.0), accs=[d_kf])
                    fw.op("act", lambda e: e.activation(out=kf[:, 1:L], in_=hb_[:, 0:L - 1], func=AF.Copy, scale=tot[:, 1:2]),
                          reads=[d_hb, d_tot], accs=[d_kf])
                    fw.op("act", lambda e: e.activation(out=kf[:, L + 1:2 * L], in_=hf[:, 1:L], func=AF.Copy, scale=tot[:, 1:2]),
                          reads=[d_hf, d_tot], accs=[d_kf])
                    fw.op("dve", lambda e: e.scalar_tensor_tensor(out=kf[:, L:L + 1], in0=hf[:, 0:1], scalar=tot[:, 1:2], in1=bcol,
                                                                  op0=ALU.mult, op1=ALU.add), reads=[d_hf, d_tot, d_hyb], accs=[d_kf])
                else:
                    fw.op("act", lambda e: e.activation(out=kf[:, 0:L - 1], in_=hf[:, 0:L - 1], func=AF.Copy, scale=tot[:, 1:2]),
                          reads=[d_hf, d_tot], accs=[d_kf])
                    fw.op("dve", lambda e: e.scalar_tensor_tensor(out=kf[:, L - 1:L], in0=hf[:, L - 1:L], scalar=tot[:, 1:2], in1=bcol,
                                                                  op0=ALU.mult, op1=ALU.add), reads=[d_hf, d_tot, d_hyb], accs=[d_kf])
                    fw.op("act", lambda e: e.activation(out=kf[:, L:2 * L - 1], in_=hb_[:, 1:L], func=AF.Copy, scale=tot[:, 1:2]),
                          reads=[d_hb, d_tot], accs=[d_kf])
                    fw.op("dve", lambda e: e.memset(kf[:, 2 * L - 1:2 * L], 0.0), accs=[d_kf])
                fw.op("dve", lambda e: e.memset(kf[:, 2 * L:2 * L + 160], 0.0), accs=[d_kf])
                store_shifted(fw, p, kf, d_kf, Fdst[order * 4 + cg], 128, 0, 2 * L)
    phase_end(fw, es)


def toep(fw, p, F, nrows, row, flen, mode, nbd, nb, rhs, d_rhs, ps, d_ps, msk_ring):
    C = flen // 2
    if mode == "A":
        base = lambda d: C - 127 + 128 * d
    else:
        base = lambda d: C - 128 - 128 * d
    bmin = min(base(-nbd), base(nbd))
    bmax = max(base(-nbd), base(nbd))
    bmin = bmin - (bmin % 32)
    Wd = ((bmax - bmin + 128 + 127) // 128) * 128
    pitch = flen + 256
    assert bmin + 96 + Wd <= pitch
    mk, d_mk = msk_ring.next()
    for a in range(4):
        src = bass.AP(tensor=F, offset=row * 32 * pitch + bmin + 32 * a, ap=[[pitch, 32], [1, Wd]])
        if a == 0:
            fw.dma("sp", mk[0:32, 0:Wd], src, writes=[d_mk])
        else:
            fw.dma("sp", mk[32 * a:32 * a + 32, 0:Wd], src, accs=[d_mk])
    order = [0] + [d for d in range(-nbd, nbd + 1) if d != 0]
    for k, d in enumerate(order):
        i0, i1 = max(0, d), min(nb, nb + d)
        if i1 <= i0:
            continue
        b0 = base(d) - bmin
        fw.op("pe", lambda e: e.matmul(ps[:, i0:i1], lhsT=mk[:, b0:b0 + 128], rhs=rhs[:, i0 - d:i1 - d],
                                       start=(k == 0), stop=(k == len(order) - 1)), reads=[d_mk, d_rhs], accs=[d_ps])


def emit_B(p, l, do_ctx):
    fw, nc = p.fw, p.nc
    W = p.W[l]
    es = phase_begin(fw)
    pj = p.pj
    bsrep, d_bs = load_rep(fw, "bsrep", W["b_short"].ap(), 1536)
    seqs = [(0, p.NL, p.FL, 2 * p.S, p.invc_lat)]
    if do_ctx:
        seqs.append((p.NL, 2, p.FC, 2 * CTX, p.invc_ctx))
    NBM = p.NL
    mskL = Ring(fw, "mskL", 2, [128, 2 * p.S], BF16)
    mskS = Ring(fw, "mskS", 3, [128, 512], BF16)
    stg = Ring(fw, "bstg", 2, [128, NBM * 128], BF16)
    U = [(fw.sb("U%d" % i, [128, 128 * NBM], BF16), Dep("U%d" % i)) for i in range(3)]
    Yg = Ring(fw, "Yg", 2, [128, NBM * 128], BF16)
    sm = Ring(fw, "bsm", 4, [128, NBM], F32)
    smb = Ring(fw, "bsmb", 4, [128, NBM], BF16)
    for (t0, nb, FLg, flen, invc_d) in seqs:
        invc = fw.sb("invc%d" % t0, [128, 4 * nb], F32)
        d_invc = Dep("invc")
        fw.dma("sp", invc[:], invc_d.ap(), writes=[d_invc])

        def load_stream(dst, d_dst, src_t, rowlen, col0):
            st, d_st = stg.next()
            for j0 in range(0, nb, 8):
                jn = min(8, nb - j0)
                src = bass.AP(tensor=src_t, offset=(t0 + j0) * 128 * rowlen + col0, ap=[[rowlen, 128], [128 * rowlen, jn], [1, 128]])
                fw.dma("sp", st[:, j0 * 128:(j0 + jn) * 128].rearrange("p (j c) -> p j c", c=128), src, accs=[d_st])
            cp(fw, "pool", dst[:, 0:128 * nb].rearrange("p (c j) -> p c j", j=nb),
               st[:, 0:nb * 128].rearrange("p (j c) -> p c j", c=128), [d_st], writes=[d_dst])

        for cg in range(4):
            load_stream(U[0][0], U[0][1], p.Y, INW, H0C + cg * 128)
            load_stream(U[1][0], U[1][1], p.YR, 1024, 512 + cg * 128)
            load_stream(U[2][0], U[2][1], p.Y, INW, H0C + 1024 + cg * 128)
            yg, d_yg = Yg.next()
            for c in range(128):
                ch = cg * 128 + c
                uv = U[0][0][:, c * nb:(c + 1) * nb]
                ux1 = U[1][0][:, c * nb:(c + 1) * nb]
                ux2 = U[2][0][:, c * nb:(c + 1) * nb]
                p1, d_p1 = pj.next()
                toep(fw, p, p.FS, 1536, 0 * 512 + ch, 512, "B", 1, nb, uv, U[0][1], p1, d_p1, mskS)
                sv, d_sv = smb.next()
                fw.op("act", lambda e: e.activation(out=sv[:, 0:nb], in_=p1[:, 0:nb], func=AF.Identity, bias=bsrep[:, ch:ch + 1]),
                      reads=[d_p1, d_bs], writes=[d_sv])
                p2, d_p2 = pj.next()
                toep(fw, p, FLg[cg], 128, c, flen, "A", nb - 1, nb, sv, d_sv, p2, d_p2, mskL)
                p3, d_p3 = pj.next()
                toep(fw, p, p.FS, 1536, 1 * 512 + ch, 512, "A", 1, nb, ux1, U[1][1], p3, d_p3, mskS)
                s1, d_s1 = sm.next()
                fw.op("act", lambda e: e.activation(out=s1[:, 0:nb], in_=p3[:, 0:nb], func=AF.Identity, bias=bsrep[:, 512 + ch:512 + ch + 1]),
                      reads=[d_p3, d_bs], writes=[d_s1])
                z, d_z = smb.next()
                fw.op("dve", lambda e: e.tensor_tensor(out=z[:, 0:nb], in0=s1[:, 0:nb], in1=p2[:, 0:nb], op=ALU.mult),
                      reads=[d_s1, d_p2], writes=[d_z])
                p4, d_p4 = pj.next()
                toep(fw, p, FLg[4 + cg], 128, c, flen, "B", nb - 1, nb, z, d_z, p4, d_p4, mskL)
                p5, d_p5 = pj.next()
                toep(fw, p, p.FS, 1536, 2 * 512 + ch, 512, "B", 1, nb, ux2, U[2][1], p5, d_p5, mskS)
                s2, d_s2 = sm.next()
                fw.op("act", lambda e: e.activation(out=s2[:, 0:nb], in_=p5[:, 0:nb], func=AF.Identity, bias=bsrep[:, 1024 + ch:1024 + ch + 1]),
                      reads=[d_p5, d_bs], writes=[d_s2])
                yv = sbap(yg, NBM * 128, c, [[128, nb]])
                fw.op("dve", lambda e: e.tensor_tensor(out=yv, in0=s2[:, 0:nb], in1=p4[:, 0:nb], op=ALU.mult),
                      reads=[d_s2, d_p4], accs=[d_yg])
            for j0 in range(0, nb, 8):
                jn = min(8, nb - j0)
                dst = bass.AP(tensor=p.BO, offset=(t0 + j0) * 128 * 1024 + 512 + cg * 128, ap=[[1024, 128], [128 * 1024, jn], [1, 128]])
                fw.dma("pool", dst, yg[:, j0 * 128:(j0 + jn) * 128].rearrange("p (j c) -> p j c", c=128), reads=[d_yg], track=d_yg)
            load_stream(U[0][0], U[0][1], p.YR, 1024, cg * 128)
            load_stream(U[1][0], U[1][1], p.Y, INW, 1536 + cg * 128)
            yg, d_yg = Yg.next()
            for c in range(128):
                ur = U[0][0][:, c * nb:(c + 1) * nb]
                un = U[1][0][:, c * nb:(c + 1) * nb]
                p1, d_p1 = pj.next()
                toep(fw, p, p.FP, 4, cg, 512, "A", 1, nb, ur, U[0][1], p1, d_p1, mskS)
                s1, d_s1 = sm.next()
                fw.op("dve", lambda e: e.tensor_tensor(out=s1[:, 0:nb], in0=p1[:, 0:nb], in1=invc[:, cg * nb:(cg + 1) * nb], op=ALU.mult),
                      reads=[d_p1, d_invc], writes=[d_s1])
                yv = sbap(yg, NBM * 128, c, [[128, nb]])
                fw.op("dve", lambda e: e.tensor_tensor(out=yv, in0=s1[:, 0:nb], in1=un, op=ALU.subtract),
                      reads=[d_s1, U[1][1]], accs=[d_yg])
            for j0 in range(0, nb, 8):
                jn = min(8, nb - j0)
                dst = bass.AP(tensor=p.BO, offset=(t0 + j0) * 128 * 1024 + cg * 128, ap=[[1024, 128], [128 * 1024, jn], [1, 128]])
                fw.dma("pool", dst, yg[:, j0 * 128:(j0 + jn) * 128].rearrange("p (j c) -> p j c", c=128), reads=[d_yg], track=d_yg)
    phase_end(fw, es)


def emit_T(p, l, do_ctx):
    fw, nc = p.fw, p.nc
    W = p.W[l]
    es = phase_begin(fw)
    pj = p.pj
    NT = p.NT
    identb, d_idb = p.identb
    lq, d_lq = load_rep(fw, "lq", W["lamqk"].ap(), 256)
    lam = fw.sb("lam", [128, 8], F32)
    d_lam = Dep("lam")
    lt = fw.sb("lqt", [128, 128], F32)
    d_lt = Dep("lqt")
    for i in range(2):
        fw.op("dve", lambda e: e.tensor_tensor(out=lt[:, i * 64:(i + 1) * 64], in0=lq[:, i * 128:i * 128 + 64], in1=lq[:, i * 128 + 64:(i + 1) * 128],
                                               op=ALU.mult), reads=[d_lq], accs=[d_lt])
        fw.op("dve", lambda e: e.tensor_reduce(out=lam[:, i:i + 1], in_=lt[:, i * 64:(i + 1) * 64], axis=AX.X, op=ALU.add),
              reads=[d_lt], accs=[d_lam])
    fw.op("act", lambda e: e.activation(out=lam[:, 2:4], in_=lam[:, 0:2], func=AF.Exp), reads=[d_lam], accs=[d_lam])
    fw.op("dve", lambda e: e.tensor_tensor(out=lam[:, 4:5], in0=lam[:, 2:3], in1=lam[:, 3:4], op=ALU.subtract), reads=[d_lam], accs=[d_lam])
    fw.op("dve", lambda e: e.tensor_scalar(out=lam[:, 5:6], in0=lam[:, 4:5], scalar1=LAM_INIT[l], scalar2=-1.0, op0=ALU.add, op1=ALU.mult),
          reads=[d_lam], accs=[d_lam])
    sg, d_sg = load_rep(fw, "subg", W["subg"].ap(), 128)
    fw.op("dve", lambda e: e.tensor_scalar(out=sg[:], in0=sg[:], scalar1=1.0 - LAM_INIT[l], scalar2=None, op0=ALU.mult),
          reads=[d_sg], writes=[d_sg])
    tok = Ring(fw, "atok", 2, [128, NT * 128], BF16)
    kT = fw.sb("kT", [128, NT * 128], BF16)
    d_kT = Dep("kT")
    qT = fw.sb("qT", [128, NT * 128], BF16)
    d_qT = Dep("qT")
    V1 = fw.sb("V1", [128, NT * 129], BF16)
    d_V1 = Dep("V1")
    ET = Ring(fw, "ET", 3, [128, 512], BF16)
    osb = Ring(fw, "osb", 2, [128, 128], F32)
    aob = Ring(fw, "aob", 2, [128, 128], BF16)
    st = Ring(fw, "ast", 2, [128, 8], F32)
    sq = fw.sb("asq", [128, 128], F32)
    d_sq = Dep("asq")
    OP = p.OP
    for h in range(4):
        for which, dst, d_dst in ((0, qT, d_qT), (1, kT, d_kT)):
            tk, d_tk = tok.next()
            for j0 in range(0, NT, 8):
                jn = min(8, NT - j0)
                src = bass.AP(tensor=p.Y, offset=j0 * 128 * INW + which * 512 + h * 128, ap=[[INW, 128], [128 * INW, jn], [1, 128]])
                fw.dma("sp", tk[:, j0 * 128:(j0 + jn) * 128].rearrange("p (j c) -> p j c", c=128), src, accs=[d_tk])
            for t4 in range(0, NT, 4):
                n = min(4, NT - t4)
                pt, d_pt = p.tpb_ps.next()
                for j in range(n):
                    fw.op("pe", lambda e: e.transpose(pt[:, j * 128:(j + 1) * 128], tk[:, (t4 + j) * 128:(t4 + j + 1) * 128], identb[:]),
                          reads=[d_tk, d_idb], accs=[d_pt])
                cp(fw, "dve", dst[:, t4 * 128:(t4 + n) * 128], pt[:, 0:n * 128], [d_pt], accs=[d_dst])
        for j0 in range(0, NT, 8):
            jn = min(8, NT - j0)
            srcv = bass.AP(tensor=p.Y, offset=j0 * 128 * INW + 1024 + h * 128, ap=[[INW, 128], [128 * INW, jn], [1, 128]])
            fw.dma("sp", sbap(V1, NT * 129, j0 * 129, [[129, jn], [1, 128]]), srcv, accs=[d_V1])
        fw.op("dve", lambda e: e.memset(sbap(V1, NT * 129, 128, [[129, NT]]), 1.0), accs=[d_V1])
        qsets = [(0, p.NL, list(range(NT)))]
        if do_ctx:
            qsets.append((p.NL, 2, [p.NL, p.NL + 1]))
        for (qt0, nqt, ktiles) in qsets:
            for qb0 in range(0, nqt, 2):
                nq = min(2, nqt - qb0)
                qc0 = (qt0 + qb0) * 128
                for ki, kt in enumerate(ktiles):
                    for m in range(2):
                        sp_, d_sp = pj.next()
                        fw.op("pe", lambda e: e.matmul(sp_[:, 0:nq * 128], lhsT=kT[m * 64:(m + 1) * 64, kt * 128:(kt + 1) * 128],
                                                       rhs=qT[m * 64:(m + 1) * 64, qc0:qc0 + nq * 128], start=True, stop=True),
                              reads=[d_kT, d_qT], writes=[d_sp])
                        et, d_et = ET.next()
                        fw.op("act", lambda e: e.activation(out=et[:, 0:nq * 128], in_=sp_[:, 0:nq * 128], func=AF.Exp, scale=0.125),
                              reads=[d_sp], writes=[d_et])
                        for qs in range(nq):
                            ot, d_ot = OP[m * 2 + qs]
                            oc = 0
                            fw.op("pe", lambda e: e.matmul(ot[:, oc:oc + 129], lhsT=et[:, qs * 128:(qs + 1) * 128],
                                                           rhs=V1[:, kt * 129:(kt + 1) * 129], start=(ki == 0), stop=(ki == len(ktiles) - 1)),
                                  reads=[d_et, d_V1], accs=[d_ot])
                for qs in range(nq):
                    s_, d_s = st.next()
                    o0t, d_o0 = OP[qs]
                    o0c = 0
                    o1t, d_o1 = OP[2 + qs]
                    o1c = 0
                    fw.op("dve", lambda e: e.reciprocal(out=s_[:, 0:1], in_=o0t[:, o0c + 128:o0c + 129]), reads=[d_o0], accs=[d_s])
                    fw.op("dve", lambda e: e.reciprocal(out=s_[:, 1:2], in_=o1t[:, o1c + 128:o1c + 129]), reads=[d_o1], accs=[d_s])
                    fw.op("dve", lambda e: e.tensor_tensor(out=s_[:, 2:3], in0=s_[:, 1:2], in1=lam[:, 5:6], op=ALU.mult), reads=[d_s, d_lam], accs=[d_s])
                    o, d_o = osb.next()
                    fw.op("dve", lambda e: e.tensor_scalar(out=o[:], in0=o0t[:, o0c:o0c + 128], scalar1=s_[:, 0:1], scalar2=None, op0=ALU.mult),
                          reads=[d_o0, d_s], writes=[d_o])
                    fw.op("dve", lambda e: e.scalar_tensor_tensor(out=o[:], in0=o1t[:, o1c:o1c + 128], scalar=s_[:, 2:3], in1=o[:],
                                                                  op0=ALU.mult, op1=ALU.add), reads=[d_o1, d_s, d_o], writes=[d_o])
                    fw.op("act", lambda e: e.activation(out=sq[:], in_=o[:], func=AF.Square, accum_out=s_[:, 3:4]), reads=[d_o], writes=[d_sq], accs=[d_s])
                    fw.op("dve", lambda e: e.tensor_scalar(out=s_[:, 4:5], in0=s_[:, 3:4], scalar1=1.0 / 128, scalar2=EPS, op0=ALU.mult, op1=ALU.add),
                          reads=[d_s], accs=[d_s])
                    fw.op("act", lambda e: e.activation(out=s_[:, 4:5], in_=s_[:, 4:5], func=AF.Sqrt), reads=[d_s], accs=[d_s])
                    fw.op("dve", lambda e: e.reciprocal(out=s_[:, 4:5], in_=s_[:, 4:5]), reads=[d_s], accs=[d_s])
                    ao, d_ao = aob.next()
                    fw.op("dve", lambda e: e.scalar_tensor_tensor(out=ao[:], in0=o[:], scalar=s_[:, 4:5], in1=sg[:], op0=ALU.mult, op1=ALU.mult),
                          reads=[d_o, d_s, d_sg], writes=[d_ao])
                    row0 = (qt0 + qb0 + qs) * 128
                    fw.dma("pool", p.AO[row0:row0 + 128, h * 128:(h + 1) * 128], ao[:], reads=[d_ao], track=d_ao)
    phase_end(fw, es)


def transpose_to(fw, p, src, d_src, ncol, dst, d_dst, ident_t, use_dve=True):
    idt, d_idt = ident_t
    nk = ncol // 128
    for k0 in range(0, nk, 4):
        n = min(4, nk - k0)
        pt, d_pt = p.tpb_ps.next()
        for j in range(n):
            fw.op("pe", lambda e: e.transpose(pt[:, j * 128:(j + 1) * 128], src[:, (k0 + j) * 128:(k0 + j + 1) * 128], idt[:]),
                  reads=[d_src, d_idt], accs=[d_pt])
        cp(fw, "dve" if use_dve else "act", dst[:, k0 * 128:(k0 + n) * 128], pt[:, 0:n * 128], [d_pt], accs=[d_dst])


def emit_C1(p, l, do_ctx):
    fw, nc = p.fw, p.nc
    W = p.W[l]
    es = phase_begin(fw)
    pj = p.pj
    es2 = ExitStack()
    fw.es = es
    wao = fw.sb("wao", [128, 4 * 1024], BF16); d_wao = Dep("wao")
    wpo = fw.sb("wpo", [128, 4 * 1024], BF16); d_wpo = Dep("wpo")
    why = fw.sb("why", [128, 4 * 1024], BF16); d_why = Dep("why")
    wout = fw.sb("wout", [128, 8 * 1024], BF16); d_wout = Dep("wout")
    wpl = fw.sb("wpl", [128, 4 * 128], BF16); d_wpl = Dep("wpl")
    GT = [(fw.sb("gate1_%d" % r, [128, D], F32), Dep("gate1_%d" % r)) for r in range(2)]
    psc = fw.sb("psc", [128, 4], F32); d_psc = Dep("psc")
    fw.dma("sp", psc[:], W["psc"].ap(), writes=[d_psc])
    fw.es = es2
    stg = Ring(fw, "wstg", 2, [128, 2048], F32)
    load_w_bf16(fw, wao, d_wao, W["w_ao"], 512, 1024, stg)
    load_w_bf16(fw, wpo, d_wpo, W["w_po"], 512, 1024, stg)
    load_w_bf16(fw, why, d_why, W["w_hy"], 512, 1024, stg)
    load_w_bf16(fw, wout, d_wout, W["w_out"], 1024, 1024, stg)
    load_w_bf16(fw, wpl, d_wpl, W["w_pool"], 512, 128, stg)
    emit_mod(fw, p.cvec, W["w_mod"], W["b_mod"], [4, 5], pj, GT, p.ones, p.ident)
    fw.barrier()
    es2.close()
    fw.es = es
    aoR = Ring(fw, "c_ao", 2, [128, 512], BF16)
    boR = Ring(fw, "c_bo", 2, [128, 1024], BF16)
    gR = Ring(fw, "c_g", 2, [128, 3072], BF16)
    hR = Ring(fw, "c_h", 2, [128, D], F32)
    aoT = fw.sb("aoT", [128, 512], BF16); d_aoT = Dep("aoT")
    pT = fw.sb("pT", [128, 512], BF16); d_pT = Dep("pT")
    yT = fw.sb("yT", [128, 512], BF16); d_yT = Dep("yT")
    mT = fw.sb("mT", [128, 512], BF16); d_mT = Dep("mT")
    mg = fw.sb("mg", [128, D], F32); d_mg = Dep("mg")
    mgb = fw.sb("mgb", [128, D], BF16); d_mgb = Dep("mgb")
    mgT = fw.sb("mgT", [128, D], BF16); d_mgT = Dep("mgT")
    tmp = Ring(fw, "c_tmp", 2, [128, 512], F32)
    hm = Ring(fw, "c_hm", 2, [128, D], F32)
    ntl = p.NT if do_ctx else p.NL
    for t in range(ntl):
        r = 1 if t >= p.NL else 0
        ao, d_ao = aoR.next()
        fw.dma("sp", ao[:], p.AO[t * 128:(t + 1) * 128, :], writes=[d_ao])
        bo, d_bo = boR.next()
        fw.dma("sp", bo[:], p.BO[t * 128:(t + 1) * 128, :], writes=[d_bo])
        g, d_g = gR.next()
        fw.dma("sp", g[:], p.Y[t * 128:(t + 1) * 128, G0C:INW], writes=[d_g])
        ht, d_ht = hR.next()
        fw.dma("sp", ht[:], p.h_src(l, t), writes=[d_ht])
        transpose_to(fw, p, ao, d_ao, 512, aoT, d_aoT, p.identb)
        transpose_to(fw, p, bo[:, 0:512], d_bo, 512, pT, d_pT, p.identb, use_dve=False)
        transpose_to(fw, p, bo[:, 512:1024], d_bo, 512, yT, d_yT, p.Jb)
        for gi in range(4):
            pt, d_pt = pj.next()
            fw.op("pe", lambda e: e.matmul(pt[:, 0:128], lhsT=wpl[:, gi * 128:(gi + 1) * 128], rhs=pT[:, gi * 128:(gi + 1) * 128],
                                           start=True, stop=True), reads=[d_wpl, d_pT], writes=[d_pt])
            fw.op("act", lambda e: e.activation(out=mT[:, gi * 128:(gi + 1) * 128], in_=pt[:, 0:128], func=AF.Copy, scale=psc[:, gi:gi + 1]),
                  reads=[d_pt, d_psc], accs=[d_mT])
        for nb in range(2):
            for bi, (xT, d_xT, wt, d_wt) in enumerate(((aoT, d_aoT, wao, d_wao), (mT, d_mT, wpo, d_wpo), (yT, d_yT, why, d_why))):
                pt, d_pt = pj.next()
                for kc in range(4):
                    fw.op("pe", lambda e: e.matmul(pt[:], lhsT=xT[:, kc * 128:(kc + 1) * 128],
                                                   rhs=wt[:, kc * 1024 + nb * 512:kc * 1024 + (nb + 1) * 512],
                                                   start=(kc == 0), stop=(kc == 3)), reads=[d_xT, d_wt], accs=[d_pt])
                gs = g[:, bi * 1024 + nb * 512:bi * 1024 + (nb + 1) * 512]
                if bi == 0:
                    fw.op("dve", lambda e: e.tensor_tensor(out=mg[:, nb * 512:(nb + 1) * 512], in0=pt[:], in1=gs, op=ALU.mult),
                          reads=[d_pt, d_g], accs=[d_mg])
                else:
                    tt, d_tt = tmp.next()
                    fw.op("dve", lambda e: e.tensor_tensor(out=tt[:], in0=pt[:], in1=gs, op=ALU.mult), reads=[d_pt, d_g], writes=[d_tt])
                    fw.op("dve", lambda e: e.tensor_tensor(out=mg[:, nb * 512:(nb + 1) * 512], in0=mg[:, nb * 512:(nb + 1) * 512], in1=tt[:], op=ALU.add),
                          reads=[d_tt, d_mg], accs=[d_mg])
        cp(fw, "act", mgb[:], mg[:], [d_mg], writes=[d_mgb])
        transpose_to(fw, p, mgb, d_mgb, 1024, mgT, d_mgT, p.identb)
        hmt, d_hm = hm.next()
        gt1, d_gt1 = GT[r]
        for nb in range(2):
            pt, d_pt = pj.next()
            for kc in range(8):
                fw.op("pe", lambda e: e.matmul(pt[:], lhsT=mgT[:, kc * 128:(kc + 1) * 128],
                                               rhs=wout[:, kc * 1024 + nb * 512:kc * 1024 + (nb + 1) * 512],
                                               start=(kc == 0), stop=(kc == 7)), reads=[d_mgT, d_wout], accs=[d_pt])
            tt, d_tt = tmp.next()
            fw.op("dve", lambda e: e.tensor_tensor(out=tt[:], in0=pt[:], in1=gt1[:, nb * 512:(nb + 1) * 512], op=ALU.mult),
                  reads=[d_pt, d_gt1], writes=[d_tt])
            fw.op("dve", lambda e: e.tensor_tensor(out=hmt[:, nb * 512:(nb + 1) * 512], in0=ht[:, nb * 512:(nb + 1) * 512], in1=tt[:], op=ALU.add),
                  reads=[d_tt, d_ht], accs=[d_hm])
        fw.dma("pool", p.HM[t * 128:(t + 1) * 128, :], hmt[:], reads=[d_hm], track=d_hm)
    phase_end(fw, es)


def emit_C2(p, l, do_ctx, last):
    fw, nc = p.fw, p.nc
    W = p.W[l]
    es = phase_begin(fw)
    pj = p.pj
    ident, d_id = p.ident
    G2 = [(fw.sb("G2_%d" % r, [128, D], F32), Dep("G2_%d" % r)) for r in range(2)]
    SH2 = [(fw.sb("SH2_%d" % r, [128, D], F32), Dep("SH2_%d" % r)) for r in range(2)]
    GA2 = [(fw.sb("GA2_%d" % r, [128, D], F32), Dep("GA2_%d" % r)) for r in range(2)]
    es2 = ExitStack()
    fw.es = es2
    MD = [(fw.sb("md2_%d" % r, [128, 3 * D], F32), Dep("md2_%d" % r)) for r in range(2)]
    emit_mod(fw, p.cvec, W["w_mod"], W["b_mod"], [6, 7, 8, 9, 10, 11], pj, MD, p.ones, p.ident)
    g2t, d_g2 = load_rep(fw, "g2t", W["g2"].ap(), D)
    for r in range(2):
        gt, d_gt = G2[r]
        mt, d_mt = MD[r]
        fw.op("dve", lambda e: e.scalar_tensor_tensor(out=gt[:], in0=mt[:, 1024:2048], scalar=1.0, in1=g2t[:],
                                                      op0=ALU.add, op1=ALU.mult), reads=[d_mt, d_g2], writes=[d_gt])
        cp(fw, "dve", SH2[r][0][:], mt[:, 0:1024], [d_mt], writes=[SH2[r][1]])
        cp(fw, "dve", GA2[r][0][:], mt[:, 2048:3072], [d_mt], writes=[GA2[r][1]])
    fw.barrier()
    es2.close()
    fw.es = es
    wfi = fw.sb("wfi", [128, 8 * 2 * FFN], BF16); d_wfi = Dep("wfi")
    wfo = fw.sb("wfo", [128, 22 * 1024], BF16); d_wfo = Dep("wfo")
    es2 = ExitStack()
    fw.es = es2
    stg = Ring(fw, "wstg", 2, [128, 2048], F32)
    load_w_bf16(fw, wfi, d_wfi, W["w_fi"], 1024, 2 * FFN, stg)
    load_w_bf16(fw, wfo, d_wfo, W["w_fo"], FFN, 1024, stg)
    fw.barrier()
    es2.close()
    fw.es = es
    fgt = None
    if last:
        fgt, d_fg = load_rep(fw, "fgt", p.final_g.ap(), D)
    hR = Ring(fw, "f_h", 2, [128, D], F32)
    a2 = fw.sb("f_a2", [128, D], F32); d_a2 = Dep("f_a2")
    a2T = fw.sb("f_a2T", [128, D], BF16); d_a2T = Dep("f_a2T")
    hT = fw.sb("f_hT", [128, 22 * 128], BF16); d_hT = Dep("f_hT")
    sg = Ring(fw, "f_sg", 2, [128, 128], F32)
    sq = fw.sb("f_sq", [128, D], F32); d_sq = Dep("f_sq")
    stat = Ring(fw, "f_st", 2, [128, 4], F32)
    tmp = Ring(fw, "f_tmp", 2, [128, 512], F32)
    hn = Ring(fw, "f_hn", 1, [128, D], F32)
    on = Ring(fw, "f_on", 1, [128, D], F32)
    ntl = p.NT if do_ctx else p.NL
    for t in range(ntl):
        r = 1 if t >= p.NL else 0
        ht, d_ht = hR.next()
        fw.dma("sp", ht[:], p.HM[t * 128:(t + 1) * 128, :], writes=[d_ht])
        st, d_st = stat.next()
        fw.op("act", lambda e: e.activation(out=sq[:], in_=ht[:], func=AF.Square, accum_out=st[:, 0:1]), reads=[d_ht], writes=[d_sq, d_st])
        fw.op("dve", lambda e: e.tensor_scalar(out=st[:, 1:2], in0=st[:, 0:1], scalar1=1.0 / D, scalar2=EPS, op0=ALU.mult, op1=ALU.add),
              reads=[d_st], accs=[d_st])
        fw.op("act", lambda e: e.activation(out=st[:, 1:2], in_=st[:, 1:2], func=AF.Sqrt), reads=[d_st], accs=[d_st])
        fw.op("dve", lambda e: e.reciprocal(out=st[:, 1:2], in_=st[:, 1:2]), reads=[d_st], accs=[d_st])
        gt, d_gt = G2[r]
        sh2, d_sh2 = SH2[r]
        ga2, d_ga2 = GA2[r]
        fw.op("dve", lambda e: e.scalar_tensor_tensor(out=a2[:], in0=ht[:], scalar=st[:, 1:2], in1=gt[:], op0=ALU.mult, op1=ALU.mult),
              reads=[d_ht, d_st, d_gt], writes=[d_a2])
        fw.op("dve", lambda e: e.tensor_add(out=a2[:], in0=a2[:], in1=sh2[:]), reads=[d_a2, d_sh2], writes=[d_a2])
        for half in range(2):
            pt, d_pt = p.tp.next()
            for j in range(4):
                kc = half * 4 + j
                fw.op("pe", lambda e: e.transpose(pt[:, j * 128:(j + 1) * 128], a2[:, kc * 128:(kc + 1) * 128], ident[:]),
                      reads=[d_a2, d_id], accs=[d_pt])
            cp(fw, "dve" if half == 0 else "act", a2T[:, half * 512:(half + 1) * 512], pt[:], [d_pt], accs=[d_a2T])
        for hc in range(22):
            pg, d_pg = pj.next()
            for which in range(2):
                c0 = which * FFN + hc * 128
                for kc in range(8):
                    fw.op("pe", lambda e: e.matmul(pg[:, which * 128:(which + 1) * 128], lhsT=wfi[:, kc * 2 * FFN + c0:kc * 2 * FFN + c0 + 128],
                                                   rhs=a2T[:, kc * 128:(kc + 1) * 128], start=(kc == 0), stop=(kc == 7)),
                          reads=[d_wfi, d_a2T], accs=[d_pg])
            s_, d_s = sg.next()
            fw.op("act", lambda e: e.activation(out=s_[:], in_=pg[:, 0:128], func=AF.Silu), reads=[d_pg], writes=[d_s])
            fw.op("dve", lambda e: e.tensor_tensor(out=hT[:, hc * 128:(hc + 1) * 128], in0=s_[:], in1=pg[:, 128:256], op=ALU.mult),
                  reads=[d_s, d_pg], accs=[d_hT])
        hnt, d_hn = hn.next()
        for nb in range(2):
            pt, d_pt = pj.next()
            for kc in range(22):
                fw.op("pe", lambda e: e.matmul(pt[:], lhsT=hT[:, kc * 128:(kc + 1) * 128],
                                               rhs=wfo[:, kc * 1024 + nb * 512:kc * 1024 + (nb + 1) * 512],
                                               start=(kc == 0), stop=(kc == 21)), reads=[d_hT, d_wfo], accs=[d_pt])
            tt, d_tt = tmp.next()
            fw.op("dve", lambda e: e.tensor_tensor(out=tt[:], in0=pt[:], in1=ga2[:, nb * 512:(nb + 1) * 512], op=ALU.mult),
                  reads=[d_pt, d_ga2], writes=[d_tt])
            fw.op("dve", lambda e: e.tensor_tensor(out=hnt[:, nb * 512:(nb + 1) * 512], in0=ht[:, nb * 512:(nb + 1) * 512], in1=tt[:], op=ALU.add),
                  reads=[d_tt, d_ht], accs=[d_hn])
        if not last:
            fw.dma("pool", p.HS[t * 128:(t + 1) * 128, :], hnt[:], reads=[d_hn], track=d_hn)
        else:
            st2, d_st2 = stat.next()
            fw.op("act", lambda e: e.activation(out=sq[:], in_=hnt[:], func=AF.Square, accum_out=st2[:, 0:1]), reads=[d_hn], writes=[d_sq, d_st2])
            fw.op("dve", lambda e: e.tensor_scalar(out=st2[:, 1:2], in0=st2[:, 0:1], scalar1=1.0 / D, scalar2=EPS, op0=ALU.mult, op1=ALU.add),
                  reads=[d_st2], accs=[d_st2])
            fw.op("act", lambda e: e.activation(out=st2[:, 1:2], in_=st2[:, 1:2], func=AF.Sqrt), reads=[d_st2], accs=[d_st2])
            fw.op("dve", lambda e: e.reciprocal(out=st2[:, 1:2], in_=st2[:, 1:2]), reads=[d_st2], accs=[d_st2])
            ot, d_ot = on.next()
            fw.op("dve", lambda e: e.scalar_tensor_tensor(out=ot[:], in0=hnt[:], scalar=st2[:, 1:2], in1=fgt[:], op0=ALU.mult, op1=ALU.mult),
                  reads=[d_hn, d_st2, d_fg], writes=[d_ot])
            fw.dma("pool", p.out[t * 128:(t + 1) * 128, :], ot[:], reads=[d_ot], track=d_ot)
    phase_end(fw, es)


LAYER_KEYS = ["w_in", "w_mod", "b_mod", "g1", "g2", "lamqk", "subg", "w_ao", "w_pool", "psc", "w_po", "w_short", "b_short",
              "hf_w1", "hf_w2", "hf_w3", "hf_par", "hyb", "w_hy", "w_out", "w_fi", "w_fo"]
LAYER_SHAPES = {"w_in": [D, INW], "w_mod": [D, 6 * D], "b_mod": [1, 6 * D], "g1": [1, D], "g2": [1, D], "lamqk": [1, 256],
                "subg": [1, 128], "w_ao": [512, D], "w_pool": [512, 128], "psc": [128, 4], "w_po": [512, D], "w_short": [3, 1536],
                "b_short": [1, 1536], "hf_w1": [33, 64], "hf_w2": [64, 64], "hf_w3": [64, 2048], "hf_par": [64, 4], "hyb": [128, 8],
                "w_hy": [512, D], "w_out": [D, D], "w_fi": [D, 2 * FFN], "w_fo": [FFN, D]}


KSTOP = 12


def build_all(S):
    p = P()
    p.S = S
    p.NL = S // 128
    p.NT = p.NL + 2
    NTOK = p.NT * 128
    nc = bass.Bass("TRN2", target_bir_lowering=False)
    p.nc = nc
    x = nc.dram_tensor("x", [S, D], F32, kind="ExternalInput")
    cx = nc.dram_tensor("ctxin", [CTX, D], F32, kind="ExternalInput")
    p.cvec = nc.dram_tensor("cvec", [2, D], F32, kind="ExternalInput")
    p.W = []
    for l in range(2):
        p.W.append({k: nc.dram_tensor("%s_%d" % (k, l), LAYER_SHAPES[k], F32, kind="ExternalInput") for k in LAYER_KEYS})
    p.final_g = nc.dram_tensor("final_g", [1, D], F32, kind="ExternalInput")
    p.cs = nc.dram_tensor("cs", [NTOK, 64], F32, kind="ExternalInput")
    p.zt_lat = nc.dram_tensor("zt_lat", [2, 33, S], F32, kind="ExternalInput")
    p.zt_ctx = nc.dram_tensor("zt_ctx", [2, 33, CTX], F32, kind="ExternalInput")
    p.negdelta = nc.dram_tensor("negdelta", [128, 4], F32, kind="ExternalInput")
    p.invc_lat = nc.dram_tensor("invc_lat", [128, 4 * p.NL], F32, kind="ExternalInput")
    p.invc_ctx = nc.dram_tensor("invc_ctx", [128, 8], F32, kind="ExternalInput")
    p.FP = nc.dram_tensor("fpool", [32 * 4, 768], BF16, kind="ExternalInput")
    p.out = nc.dram_tensor("out", [S, D], F32, kind="ExternalOutput")
    p.Y = nc.dram_tensor("Ysc", [NTOK, INW], BF16)
    p.YR = nc.dram_tensor("YRsc", [NTOK, 1024], BF16)
    p.FS = nc.dram_tensor("FSsc", [32 * 1536, 768], BF16)
    p.FL = [nc.dram_tensor("FLsc%d" % i, [32 * 128, 2 * S + 256], BF16) for i in range(8)]
    p.FC = [nc.dram_tensor("FCsc%d" % i, [32 * 128, 2 * CTX + 256], BF16) for i in range(8)]
    p.BO = nc.dram_tensor("BOsc", [NTOK, 1024], BF16)
    p.AO = nc.dram_tensor("AOsc", [NTOK, 512], BF16)
    p.HM = nc.dram_tensor("HMsc", [NTOK, D], F32)
    p.HS = nc.dram_tensor("HSsc", [NTOK, D], F32)

    def h_src(l, t):
        if l == 0:
            if t < p.NL:
                return x[t * 128:(t + 1) * 128, :]
            return cx[(t - p.NL) * 128:(t - p.NL + 1) * 128, :]
        return p.HS[t * 128:(t + 1) * 128, :]
    p.h_src = h_src
    fw = FW(nc)
    p.fw = fw
    p.ident = make_ident(fw)
    ident, d_id = p.ident
    ones = fw.sb("ones", [128, 128], F32)
    d_ones = Dep("ones")
    fw.op("pool", lambda e: e.memset(ones[:], 1.0), writes=[d_ones])
    p.ones = (ones, d_ones)
    identb = fw.sb("identb", [128, 128], BF16)
    d_idb = Dep("identb")
    cp(fw, "dve", identb[:], ident[:], [d_id], writes=[d_idb])
    p.identb = (identb, d_idb)
    Jf = fw.sb("Jf", [128, 128], F32)
    d_Jf = Dep("Jf")
    fw.op("pool", lambda e: e.memset(Jf[:], 0.0), writes=[d_Jf])
    fw.op("pool", lambda e: e.affine_select(out=Jf[:], in_=Jf[:], pattern=[[1, 128]], compare_op=ALU.not_equal, fill=1.0,
                                            base=-127, channel_multiplier=1), reads=[d_Jf], writes=[d_Jf])
    Jb = fw.sb("Jb", [128, 128], BF16)
    d_Jb = Dep("Jb")
    cp(fw, "dve", Jb[:], Jf[:], [d_Jf], writes=[d_Jb])
    p.Jb = (Jb, d_Jb)
    p.pj = Ring(fw, "pj", 2, [128, 512], F32, psum=True)
    p.tp = Ring(fw, "tp", 1, [128, 512], F32, psum=True)
    p.tpb_ps = Ring(fw, "tpb", 1, [128, 512], BF16, psum=True)
    p.OP = [(fw.ps("OP%d" % i, [128, 512], F32), Dep("OP%d" % i)) for i in range(4)]
    for l in range(2):
        last = (l == 1)
        do_ctx = not last
        for ph, fn in enumerate((lambda: emit_A(p, l), lambda: emit_F(p, l, do_ctx), lambda: emit_B(p, l, do_ctx),
                                 lambda: emit_T(p, l, do_ctx), lambda: emit_C1(p, l, do_ctx), lambda: emit_C2(p, l, do_ctx, last))):
            if l * 6 + ph < KSTOP:
                fn()
    return nc


def rope_tables(S):
    rows = S // 64
    r = np.arange(rows, dtype=np.float32)
    cidx = np.arange(64, dtype=np.float32)
    row = np.broadcast_to(r[:, None], (rows, 64)).reshape(-1)
    col = np.broadcast_to(cidx[None, :], (rows, 64)).reshape(-1)
    inv = (1.0 / (np.float32(10000.0) ** (np.arange(16, dtype=np.float32) * np.float32(2.0) / np.float32(32)))).astype(np.float32)
    ang = np.stack([row[:, None] * inv, col[:, None] * inv], axis=1).astype(np.float32)
    return np.cos(ang).astype(np.float32).reshape(S, 32), np.sin(ang).astype(np.float32).reshape(S, 32)


def pos_table(l):
    f32 = np.float32
    t01 = np.linspace(0.0, 1.0, l, dtype=f32)
    tr = np.arange(l, dtype=f32)
    bands = np.linspace(1e-4, 15, 16, dtype=f32)
    ang = (f32(2.0 * math.pi) * tr[:, None] * bands[None, :] / f32(l)).astype(f32)
    z = np.concatenate([t01[:, None], np.cos(ang), np.sin(ang)], axis=-1).astype(f32)
    zt = np.ascontiguousarray(z.T)
    return np.ascontiguousarray(np.stack([zt, zt[:, ::-1]], axis=0))


def invc_table(L):
    nb = L // 128
    t = np.arange(L)
    out = np.zeros((128, 4 * nb), np.float32)
    for g, w in enumerate(POOL_WIN):
        cnt = (np.minimum(t + w // 2, L) - np.maximum(t - w // 2, 0)).astype(np.float32)
        out[:, g * nb:(g + 1) * nb] = (1.0 / cnt).reshape(nb, 128).T
    return out


_NC_CACHE = {}


def kernel(**inp):
    inp = {k: np.asarray(v) for k, v in inp.items()}
    x = inp["x"]
    B, S, _ = x.shape
    if S not in _NC_CACHE:
        _NC_CACHE[S] = build_all(S)
    nc = _NC_CACHE[S]
    cos, sin = rope_tables(S)
    cs = np.zeros((S + CTX, 64), np.float32)
    cs[:S, 0:32] = cos
    cs[:S, 32:64] = sin
    cs[S:, 0:32] = 1.0
    max_decay = math.log(1e-2) / 0.3
    min_decay = math.log(1e-2) / 1.5
    deltas = np.abs(np.linspace(min_decay, max_decay, 512, dtype=np.float32))
    negdelta = np.ascontiguousarray((-deltas).reshape(4, 128).T).astype(np.float32)
    fpool0 = np.zeros((4, 768 + 32), np.float32)
    for g, w in enumerate(POOL_WIN):
        fpool0[g, 256 - (w // 2 - 1):256 + w // 2 + 1] = 1.0
    fpool = np.stack([fpool0[:, b:b + 768] for b in range(32)], axis=1).reshape(4 * 32, 768)
    common = {"cs": cs, "zt_lat": pos_table(S), "zt_ctx": pos_table(CTX), "negdelta": negdelta,
              "invc_lat": invc_table(S), "invc_ctx": invc_table(CTX), "fpool": fpool.astype(NPBF),
              "final_g": np.ascontiguousarray(inp["final_g"][None, :])}
    for l in range(2):
        lw = {
            "w_in": inp["w_in"][l], "w_mod": inp["w_mod"][l], "b_mod": inp["b_mod"][l][None, :], "g1": inp["norm1_g"][l][None, :],
            "g2": inp["norm2_g"][l][None, :], "lamqk": inp["lam_qk"][l].reshape(1, 256), "subg": inp["subln_g"][l][None, :],
            "w_ao": inp["w_attn_o"][l], "w_pool": inp["w_pool"][l].reshape(512, 128), "psc": inp["pool_scale"][l].reshape(4, 128).T,
            "w_po": inp["w_pool_o"][l], "w_short": inp["w_short"][l], "b_short": inp["b_short"][l][None, :],
            "hf_w1": inp["hf_w1"][l], "hf_w2": inp["hf_w2"][l], "hf_w3": inp["hf_w3"][l],
            "hf_par": np.stack([inp["hf_b1"][l], inp["hf_freq1"][l], inp["hf_b2"][l], inp["hf_freq2"][l]], axis=1),
            "hyb": inp["hy_bias"][l].reshape(2, 4, 128).transpose(2, 0, 1).reshape(128, 8),
            "w_hy": inp["w_hy_o"][l], "w_out": inp["w_out"][l], "w_fi": inp["w_ffn_in"][l], "w_fo": inp["w_ffn_out"][l],
        }
        for k, v in lw.items():
            common["%s_%d" % (k, l)] = np.ascontiguousarray(v, dtype=np.float32)
    in_maps = []
    for b in range(B):
        m = dict(common)
        m["x"] = np.ascontiguousarray(x[b])
        m["ctxin"] = np.ascontiguousarray(inp["ctx"][b])
        m["cvec"] = np.ascontiguousarray(np.stack([inp["c"][b], inp["c_ctx"]], axis=0))
        in_maps.append(m)
    res = run_bass_kernel_spmd(nc, in_maps, core_ids=list(range(B)))
    return np.stack([r["out"] for r in res.results], axis=0).astype(np.float32)
```

```python
import math
from contextlib import ExitStack
import numpy as np
import ml_dtypes
import concourse.bass as bass
import concourse.mybir as mybir
from concourse.bass_utils import run_bass_kernel_spmd

F32 = mybir.dt.float32
BF16 = mybir.dt.bfloat16
AF = mybir.ActivationFunctionType
ALU = mybir.AluOpType
AX = mybir.AxisListType
NPBF = ml_dtypes.bfloat16

D = 1024
INW = 6656
NMOD = 6
FFN = 2816
CTX = 256
EPS = 1e-6
SELF_SYNC = True


class Dep:
    __slots__ = ("name", "writers", "readers", "dsem", "dcnt")

    def __init__(self, name=""):
        self.name = name
        self.writers = {}
        self.readers = {}
        self.dsem = {}
        self.dcnt = {}


class FW:
    def __init__(self, nc):
        self.nc = nc
        self.E = {"pe": nc.tensor, "dve": nc.vector, "act": nc.scalar, "pool": nc.gpsimd, "sp": nc.sync}
        self.sems = {}
        self.cnt = {}
        self.semobj = {}
        for k in ("pe", "dve", "act", "pool"):
            self.sems[k] = nc.alloc_semaphore(name="c_" + k)
            self.cnt[k] = 0
            self.semobj[("c", k)] = self.sems[k]
        self.waited = {k: {} for k in self.E}
        self.ndsem = 0
        self.es = None
        self.uid = 0
        self.free_dsem = {}
        self.all_dma_deps = []
        self.n_inst = 0
        self.n_wait = 0

    def sb(self, name, shape, dt=F32):
        self.uid += 1
        name = "%s_u%d" % (name, self.uid)
        if self.es is not None:
            return self.es.enter_context(self.nc.sbuf_tensor(name, list(shape), dt))
        return self.nc.alloc_sbuf_tensor(name, list(shape), dt)

    def barrier(self):
        needs = {("c", k): v for k, v in self.cnt.items() if v > 0}
        for d, q in self.all_dma_deps:
            if d.dcnt[q] > 0:
                needs[d.dsem[q]] = d.dcnt[q]
        for eng in self.E:
            self._wait(eng, dict(needs))
        for d, q in self.all_dma_deps:
            self.free_dsem.setdefault(q, []).append((d.dsem[q], d.dcnt[q]))
            del d.dsem[q]
            del d.dcnt[q]
        self.all_dma_deps = []

    def ps(self, name, shape, dt=F32):
        return self.nc.alloc_psum_tensor(name, list(shape), dt)

    def _dsem(self, d, q):
        if q not in d.dsem:
            fp = self.free_dsem.setdefault(q, [])
            if fp:
                d.dsem[q], d.dcnt[q] = fp.pop()
            else:
                d.dsem[q] = ("d", self.ndsem)
                self.semobj[d.dsem[q]] = self.nc.alloc_semaphore(name="d_%d" % self.ndsem)
                self.ndsem += 1
                d.dcnt[q] = 0
            self.all_dma_deps.append((d, q))
        return d.dsem[q]

    def _wait(self, eng, needs):
        e = self.E[eng]
        w = self.waited[eng]
        for key, val in needs.items():
            if key == ("c", eng) and (eng == "pe" or not SELF_SYNC):
                continue
            if w.get(key, 0) >= val:
                continue
            e.wait_ge(self.semobj[key], val)
            w[key] = val
            self.n_wait += 1

    @staticmethod
    def _needs(reads, writes, accs):
        needs = {}
        for d in reads:
            for k, v in d.writers.items():
                if needs.get(k, 0) < v:
                    needs[k] = v
        for d in list(writes) + list(accs):
            for src in (d.writers, d.readers):
                for k, v in src.items():
                    if needs.get(k, 0) < v:
                        needs[k] = v
        return needs

    @staticmethod
    def _commit(tok, reads, writes, accs):
        k, v = tok
        for d in reads:
            if d.readers.get(k, 0) < v:
                d.readers[k] = v
        for d in writes:
            d.writers = {k: v}
            d.readers = {}
        for d in accs:
            if d.writers.get(k, 0) < v:
                d.writers[k] = v
            d.readers = {}

    def op(self, eng, emit, reads=(), writes=(), accs=()):
        self._wait(eng, self._needs(reads, writes, accs))
        ins = emit(self.E[eng])
        self.cnt[eng] += 1
        ins.then_inc(self.sems[eng], 1)
        self.n_inst += 1
        self._commit((("c", eng), self.cnt[eng]), reads, writes, accs)
        return ins

    def dma(self, q, out, in_, reads=(), writes=(), accs=(), track=None, **kw):
        self._wait(q, self._needs(reads, writes, accs))
        if track is None:
            track = (list(writes) + list(accs) + list(reads))[0]
        key = self._dsem(track, q)
        ins = self.E[q].dma_start(out=out, in_=in_, **kw)
        track.dcnt[q] += 16
        ins.then_inc(self.semobj[key], 16)
        self.n_inst += 1
        self._commit((key, track.dcnt[q]), reads, writes, accs)
        return ins

    def finish(self, deps, eng="sp"):
        needs = {}
        for d in deps:
            for src in (d.writers, d.readers):
                for k, v in src.items():
                    if needs.get(k, 0) < v:
                        needs[k] = v
        self._wait(eng, needs)


class Ring:
    def __init__(self, fw, name, n, shape, dt=F32, psum=False):
        self.t = []
        self.d = []
        for i in range(n):
            nm = "%s%d" % (name, i)
            self.t.append(fw.ps(nm, shape, dt) if psum else fw.sb(nm, shape, dt))
            self.d.append(Dep(nm))
        self.i = 0
        self.n = n

    def next(self):
        i = self.i
        self.i = (i + 1) % self.n
        return self.t[i], self.d[i]


class RingOf:
    def __init__(self, pairs):
        self.t = [a for a, _ in pairs]
        self.d = [b for _, b in pairs]
        self.i = 0
        self.n = len(pairs)

    def next(self):
        i = self.i
        self.i = (i + 1) % self.n
        return self.t[i], self.d[i]


def sbap(t, rowlen, offset, dims, npart=128):
    return bass.AP(tensor=t, offset=offset, ap=[[rowlen, npart]] + [list(x) for x in dims])


def make_ident(fw, name="ident"):
    ident = fw.sb(name, [128, 128], F32)
    d = Dep(name)
    fw.op("pool", lambda e: e.memset(ident[:], 0.0), writes=[d])
    fw.op("pool", lambda e: e.affine_select(out=ident[:], in_=ident[:], pattern=[[-1, 128]],
                                            compare_op=ALU.not_equal, fill=1.0, base=0, channel_multiplier=1),
          reads=[d], writes=[d])
    return ident, d


def emit_mod(fw, cvec, w_mod, b_mod, cols, pj, out_tiles, ident_ones, ident_t):
    nc = fw.nc
    ones, d_ones = ident_ones
    ident, d_id = ident_t
    cR = fw.sb("cR", [16, 128], F32)
    d_cR = Dep("cR")
    fw.dma("sp", cR[:], cvec.ap().rearrange("r (k p) -> (r k) p", p=128), writes=[d_cR])
    pt, d_pt = pj.next()
    fw.op("pe", lambda e: e.transpose(pt[:, 0:16], cR[:], ident[0:16, 0:16]), reads=[d_cR, d_id], writes=[d_pt])
    cA = fw.sb("cA", [128, 16], F32)
    d_cA = Dep("cA")
    fw.op("act", lambda e: e.activation(out=cA[:], in_=pt[:, 0:16], func=AF.Silu), reads=[d_pt], writes=[d_cA])
    rep = fw.sb("crep", [128, 16 * 128], F32)
    d_rep = Dep("crep")
    for i in range(16):
        fw.op("dve", lambda e, i=i: e.tensor_scalar(out=rep[:, i * 128:(i + 1) * 128], in0=ones[:, 0:128],
                                                     scalar1=cA[:, i:i + 1], scalar2=None, op0=ALU.mult),
              reads=[d_cA, d_ones], accs=[d_rep])
    c_lo = cols[0] * 512
    bm = fw.sb("bmod", [1, len(cols) * 512], F32)
    d_bm = Dep("bmod")
    fw.dma("sp", bm[:], b_mod[:, c_lo:c_lo + len(cols) * 512], writes=[d_bm])
    wst = Ring(fw, "wmst", 3, [128, 512], F32)
    for r in range(2):
        ot, d_ot = out_tiles[r]
        for j, nb in enumerate(cols):
            pt, d_pt = pj.next()
            for kc in range(8):
                wt, d_wt = wst.next()
                fw.dma("sp", wt[:], w_mod[kc * 128:(kc + 1) * 128, nb * 512:(nb + 1) * 512], writes=[d_wt])
                fw.op("pe", lambda e, kc=kc, wt=wt, pt=pt, r=r: e.matmul(
                    pt[:], lhsT=rep[:, (r * 8 + kc) * 128:(r * 8 + kc + 1) * 128], rhs=wt[:],
                    start=(kc == 0), stop=False), reads=[d_rep, d_wt], accs=[d_pt])
            fw.op("pe", lambda e, pt=pt, nb=nb: e.matmul(pt[:], lhsT=ones[0:1, 0:128], rhs=bm[0:1, nb * 512 - c_lo:(nb + 1) * 512 - c_lo],
                                                        start=False, stop=True), reads=[d_ones, d_bm], accs=[d_pt])
            fw.op("act", lambda e, pt=pt, ot=ot, j=j: e.copy(out=ot[:, j * 512:(j + 1) * 512], in_=pt[:]),
                  reads=[d_pt], accs=[d_ot])


def load_rep(fw, name, src_row_ap, n, q="sp"):
    t = fw.sb(name, [128, n], F32)
    d = Dep(name)
    fw.dma(q, t[:], src_row_ap.partition_broadcast(128), writes=[d])
    return t, d


def load_w_bf16(fw, wb, d_wb, w, K, N, stage_ring, chunk=2048):
    kc_n = K // 128
    i = 0
    for kc in range(kc_n):
        for c0 in range(0, N, chunk):
            c1 = min(N, c0 + chunk)
            st, d_st = stage_ring.next()
            fw.dma("sp", st[:, 0:c1 - c0], w[kc * 128:(kc + 1) * 128, c0:c1], writes=[d_st])
            eng = "act" if (i % 2 == 0) else "dve"
            if eng == "act":
                fw.op("act", lambda e, st=st, kc=kc, c0=c0, c1=c1: e.copy(out=wb[:, kc * N + c0:kc * N + c1], in_=st[:, 0:c1 - c0]),
                      reads=[d_st], accs=[d_wb])
            else:
                fw.op("dve", lambda e, st=st, kc=kc, c0=c0, c1=c1: e.tensor_copy(out=wb[:, kc * N + c0:kc * N + c1], in_=st[:, 0:c1 - c0]),
                      reads=[d_st], accs=[d_wb])
            i += 1
    return


H0C = 2048
G0C = 3584
LAM_INIT = [0.8 - 0.6 * math.exp(-0.3 * l) for l in range(2)]
MAGIC = 12582912.0
POOL_WIN = (2, 4, 8, 16)


class P:
    pass


def cp(fw, eng, out, in_, reads, writes=(), accs=()):
    if eng == "act":
        return fw.op("act", lambda e: e.copy(out=out, in_=in_), reads=reads, writes=writes, accs=accs)
    return fw.op(eng, lambda e: e.tensor_copy(out=out, in_=in_), reads=reads, writes=writes, accs=accs)


def phase_begin(fw):
    es = ExitStack()
    fw.es = es
    return es


def phase_end(fw, es):
    fw.barrier()
    fw.es = None
    es.close()


def emit_A(p, l):
    fw, nc = p.fw, p.nc
    W = p.W[l]
    es = phase_begin(fw)
    pj = p.pjbig
    tp = p.tp
    ident, d_id = p.ident
    wb = fw.sb("w_in_b", [128, 8 * INW], BF16)
    d_wb = Dep("w_in_b")
    G = [(fw.sb("G%d" % r, [128, D], F32), Dep("G%d" % r)) for r in range(2)]
    SH = [(fw.sb("SH%d" % r, [128, D], F32), Dep("SH%d" % r)) for r in range(2)]
    es2 = ExitStack()
    fw.es = es2
    modt = [(fw.sb("mod%d" % r, [128, 2048], F32), Dep("mod%d" % r)) for r in range(2)]
    emit_mod(fw, p.cvec, W["w_mod"], W["b_mod"], [0, 1, 2, 3], pj, modt, p.ones, p.ident)
    g1t, d_g1 = load_rep(fw, "g1t", W["g1"].ap(), D)
    for r in range(2):
        gt, d_gt = G[r]
        sh, d_sh = SH[r]
        mt, d_mt = modt[r]
        fw.op("dve", lambda e: e.scalar_tensor_tensor(out=gt[:], in0=mt[:, 1024:2048], scalar=1.0, in1=g1t[:],
                                                      op0=ALU.add, op1=ALU.mult), reads=[d_mt, d_g1], writes=[d_gt])
        cp(fw, "dve", sh[:], mt[:, 0:1024], [d_mt], writes=[d_sh])
    stg = Ring(fw, "wstg", 2, [128, 2048], F32)
    load_w_bf16(fw, wb, d_wb, W["w_in"], D, INW, stg)
    fw.barrier()
    es2.close()
    fw.es = es
    hbuf = Ring(fw, "hbuf", 2, [128, D], F32)
    abuf = Ring(fw, "abuf", 1, [128, D], F32)
    aT = Ring(fw, "aT", 2, [128, 8 * 128], BF16)
    ybuf = Ring(fw, "ybuf", 2, [128, INW], BF16)
    yrb = Ring(fw, "yrb", 2, [128, 1024], BF16)
    qkf = Ring(fw, "qkf", 1, [128, 1024], F32)
    csb = Ring(fw, "csb", 2, [128, 64], F32)
    tmp = Ring(fw, "rtmp", 2, [128, 512], F32)
    sq = fw.sb("sq", [128, D], F32)
    d_sq = Dep("sq")
    stat = Ring(fw, "stat", 2, [128, 2], F32)
    Jb, d_J = p.Jb
    for t in range(p.NT):
        r = 1 if t >= p.NL else 0
        ht, d_ht = hbuf.next()
        fw.dma("sp", ht[:], p.h_src(l, t), writes=[d_ht])
        ct, d_ct = csb.next()
        fw.dma("sp", ct[:], p.cs[t * 128:(t + 1) * 128, :], writes=[d_ct])
        st, d_st = stat.next()
        fw.op("act", lambda e: e.activation(out=sq[:], in_=ht[:], func=AF.Square, accum_out=st[:, 0:1]),
              reads=[d_ht], writes=[d_sq, d_st])
        fw.op("dve", lambda e: e.tensor_scalar(out=st[:, 1:2], in0=st[:, 0:1], scalar1=1.0 / D, scalar2=EPS,
                                               op0=ALU.mult, op1=ALU.add), reads=[d_st], accs=[d_st])
        fw.op("act", lambda e: e.activation(out=st[:, 1:2], in_=st[:, 1:2], func=AF.Sqrt), reads=[d_st], accs=[d_st])
        fw.op("dve", lambda e: e.reciprocal(out=st[:, 1:2], in_=st[:, 1:2]), reads=[d_st], accs=[d_st])
        at, d_at = abuf.next()
        gt, d_gt = G[r]
        sh, d_sh = SH[r]
        fw.op("dve", lambda e: e.scalar_tensor_tensor(out=at[:], in0=ht[:], scalar=st[:, 1:2], in1=gt[:],
                                                      op0=ALU.mult, op1=ALU.mult), reads=[d_ht, d_st, d_gt], writes=[d_at])
        fw.op("dve", lambda e: e.tensor_add(out=at[:], in0=at[:], in1=sh[:]), reads=[d_at, d_sh], writes=[d_at])
        aTt, d_aT = aT.next()
        for half in range(2):
            pt, d_pt = tp.next()
            for j in range(4):
                kc = half * 4 + j
                fw.op("pe", lambda e: e.transpose(pt[:, j * 128:(j + 1) * 128], at[:, kc * 128:(kc + 1) * 128], ident[:]),
                      reads=[d_at, d_id], accs=[d_pt])
            cp(fw, "dve" if half == 0 else "act", aTt[:, half * 512:(half + 1) * 512], pt[:], [d_pt], accs=[d_aT])
        yt, d_yt = ybuf.next()
        qt, d_qt = qkf.next()
        for nb in range(13):
            pt, d_pt = pj.next()
            for kc in range(8):
                fw.op("pe", lambda e: e.matmul(pt[:], lhsT=aTt[:, kc * 128:(kc + 1) * 128],
                                               rhs=wb[:, kc * INW + nb * 512:kc * INW + (nb + 1) * 512],
                                               start=(kc == 0), stop=(kc == 7)), reads=[d_aT, d_wb], accs=[d_pt])
            if nb < 2:
                cp(fw, "act", qt[:, nb * 512:(nb + 1) * 512], pt[:], [d_pt], accs=[d_qt])
            elif nb < 7:
                cp(fw, "act", yt[:, nb * 512:(nb + 1) * 512], pt[:], [d_pt], accs=[d_yt])
            else:
                fw.op("act", lambda e: e.activation(out=yt[:, nb * 512:(nb + 1) * 512], in_=pt[:], func=AF.Sigmoid),
                      reads=[d_pt], accs=[d_yt])
            if nb == 1:
                t1, d_t1 = tmp.next()
                t2, d_t2 = tmp.next()
                xa = sbap(qt, 1024, 0, [[64, 16], [32, 2], [1, 16]])
                xb = sbap(qt, 1024, 16, [[64, 16], [32, 2], [1, 16]])
                cosv = sbap(ct, 64, 0, [[0, 16], [16, 2], [1, 16]])
                sinv = sbap(ct, 64, 32, [[0, 16], [16, 2], [1, 16]])
                oa = sbap(yt, INW, 0, [[64, 16], [32, 2], [1, 16]])
                ob = sbap(yt, INW, 16, [[64, 16], [32, 2], [1, 16]])
                v1 = sbap(t1, 512, 0, [[32, 16], [16, 2], [1, 16]])
                v2 = sbap(t2, 512, 0, [[32, 16], [16, 2], [1, 16]])
                fw.op("dve", lambda e: e.tensor_tensor(out=v1, in0=xa, in1=cosv, op=ALU.mult), reads=[d_qt, d_ct], writes=[d_t1])
                fw.op("dve", lambda e: e.tensor_tensor(out=v2, in0=xb, in1=sinv, op=ALU.mult), reads=[d_qt, d_ct], writes=[d_t2])
                fw.op("dve", lambda e: e.tensor_tensor(out=oa, in0=v1, in1=v2, op=ALU.subtract), reads=[d_t1, d_t2], accs=[d_yt])
                fw.op("dve", lambda e: e.tensor_tensor(out=v1, in0=xb, in1=cosv, op=ALU.mult), reads=[d_qt, d_ct], writes=[d_t1])
                fw.op("dve", lambda e: e.tensor_tensor(out=v2, in0=xa, in1=sinv, op=ALU.mult), reads=[d_qt, d_ct], writes=[d_t2])
                fw.op("dve", lambda e: e.tensor_tensor(out=ob, in0=v1, in1=v2, op=ALU.add), reads=[d_t1, d_t2], accs=[d_yt])
        fw.dma("pool", p.Y[t * 128:(t + 1) * 128, :], yt[:], reads=[d_yt], track=d_yt)
        yr, d_yr = yrb.next()
        for k, c0 in enumerate((1536, H0C + 512)):
            pt, d_pt = pj.next()
            fw.op("pe", lambda e: e.matmul(pt[:], lhsT=Jb[:], rhs=yt[:, c0:c0 + 512], start=True, stop=True),
                  reads=[d_J, d_yt], writes=[d_pt])
            cp(fw, "dve", yr[:, k * 512:(k + 1) * 512], pt[:], [d_pt], accs=[d_yr])
        fw.dma("pool", p.YR[t * 128:(t + 1) * 128, :], yr[:], reads=[d_yr], track=d_yr)
    phase_end(fw, es)


def sin_layer(fw, p, ps, nrows, bcol, fcol, out, d_out, ncol, rd):
    t1, d_t1 = p.sring.next()
    t2, d_t2 = p.sring.next()
    fw.op("dve", lambda e: e.tensor_scalar(out=t1[0:nrows, 0:ncol], in0=ps, scalar1=bcol, scalar2=fcol,
                                           op0=ALU.add, op1=ALU.mult), reads=rd, writes=[d_t1])
    fw.op("dve", lambda e: e.tensor_scalar(out=t2[0:nrows, 0:ncol], in0=t1[0:nrows, 0:ncol], scalar1=MAGIC, scalar2=MAGIC,
                                           op0=ALU.add, op1=ALU.subtract), reads=[d_t1], writes=[d_t2])
    fw.op("dve", lambda e: e.tensor_tensor(out=t1[0:nrows, 0:ncol], in0=t1[0:nrows, 0:ncol], in1=t2[0:nrows, 0:ncol],
                                           op=ALU.subtract), reads=[d_t1, d_t2], writes=[d_t1])
    fw.op("act", lambda e: e.activation(out=out, in_=t1[0:nrows, 0:ncol], func=AF.Sin, scale=2 * math.pi),
          reads=[d_t1], accs=[d_out])


def emit_F(p, l, do_ctx):
    fw, nc = p.fw, p.nc
    W = p.W[l]
    es = phase_begin(fw)
    ident, d_id = p.ident
    ones, d_ones = p.ones
    pj = p.pjbig
    def ld(name, src, shape):
        t = fw.sb(name, shape, F32)
        d = Dep(name)
        fw.dma("sp", t[:], src, writes=[d])
        return t, d
    w1, d_w1 = ld("hw1", W["hf_w1"].ap(), [33, 64])
    w2, d_w2 = ld("hw2", W["hf_w2"].ap(), [64, 64])
    w3, d_w3 = ld("hw3", W["hf_w3"].ap(), [64, 2048])
    pr, d_pr = ld("hpr", W["hf_par"].ap(), [64, 4])
    nd, d_nd = ld("negd", p.negdelta.ap(), [128, 4])
    hyb, d_hyb = ld("hyb", W["hyb"].ap(), [128, 8])
    fdiv = fw.sb("fdiv", [64, 2], F32)
    d_fd = Dep("fdiv")
    fw.op("dve", lambda e: e.tensor_scalar(out=fdiv[:, 0:1], in0=pr[:, 1:2], scalar1=1.0 / (2 * math.pi), scalar2=None, op0=ALU.mult),
          reads=[d_pr], accs=[d_fd])
    fw.op("dve", lambda e: e.tensor_scalar(out=fdiv[:, 1:2], in0=pr[:, 3:4], scalar1=1.0 / (2 * math.pi), scalar2=None, op0=ALU.mult),
          reads=[d_pr], accs=[d_fd])
    p.sring = Ring(fw, "sring", 2, [128, 512], F32)
    wsr, d_wsr = ld("wsr", W["w_short"].ap(), [3, 1536])
    fst = fw.sb("fst", [128, 512], BF16)
    d_fst = Dep("fst")
    for g in range(12):
        stream = g // 4
        pt, d_pt = pj.next()
        fw.op("pe", lambda e: e.transpose(pt[:, 0:3], wsr[0:3, g * 128:(g + 1) * 128], ident[0:3, 0:3]),
              reads=[d_wsr, d_id], writes=[d_pt])
        fw.op("dve", lambda e: e.memset(fst[:], 0.0), writes=[d_fst])
        for j in range(3):
            idx = (257 - j) if stream == 1 else (254 + j)
            cp(fw, "dve", fst[:, idx:idx + 1], pt[:, j:j + 1], [d_pt], accs=[d_fst])
        fw.dma("sp", p.FS[g * 128:(g + 1) * 128, 0:512], fst[:], reads=[d_fst], track=d_fst)
    hd = [(fw.sb("hd%d" % i, [128, p.S], F32), Dep("hd%d" % i)) for i in range(2)]
    kf = fw.sb("kfrow", [128, 2 * p.S], BF16)
    d_kf = Dep("kfrow")
    h1 = fw.sb("h1", [64, 512], F32)
    d_h1 = Dep("h1")
    h2 = fw.sb("h2", [64, 512], F32)
    d_h2 = Dep("h2")
    ztile = Ring(fw, "ztile", 2, [33, 512], F32)
    dec = Ring(fw, "dec", 2, [128, 512], F32)
    asum = fw.sb("asum", [128, 64], F32)
    d_as = Dep("asum")
    tot = fw.sb("tot", [128, 4], F32)
    d_tot = Dep("tot")
    seqs = [("lat", p.S, p.zt_lat, p.FL)]
    if do_ctx:
        seqs.append(("ctx", CTX, p.zt_ctx, p.FC))
    for (sname, L, zt, Fdst) in seqs:
        cb = min(512, L)
        ncb = L // cb
        for order in range(2):
            for cg in range(4):
                fw.op("dve", lambda e: e.memset(asum[:], 0.0), writes=[d_as])
                for dr in range(2):
                    rev = (dr == 1) if order == 0 else (dr == 0)
                    hdt, d_hd = hd[dr]
                    col0 = dr * 1024 + order * 512 + cg * 128
                    for b in range(ncb):
                        z, d_z = ztile.next()
                        fw.dma("sp", z[:, 0:cb], zt[1 if rev else 0, :, b * cb:(b + 1) * cb], writes=[d_z])
                        pa, d_pa = pj.next()
                        fw.op("pe", lambda e: e.matmul(pa[0:64, 0:cb], lhsT=w1[:, :], rhs=z[:, 0:cb], start=True, stop=True),
                              reads=[d_w1, d_z], writes=[d_pa])
                        sin_layer(fw, p, pa[0:64, 0:cb], 64, pr[:, 0:1], fdiv[:, 0:1], h1[:, 0:cb], d_h1, cb, [d_pa, d_pr, d_fd])
                        pb, d_pb = pj.next()
                        fw.op("pe", lambda e: e.matmul(pb[0:64, 0:cb], lhsT=w2[:, :], rhs=h1[:, 0:cb], start=True, stop=True),
                              reads=[d_w2, d_h1], writes=[d_pb])
                        sin_layer(fw, p, pb[0:64, 0:cb], 64, pr[:, 2:3], fdiv[:, 1:2], h2[:, 0:cb], d_h2, cb, [d_pb, d_pr, d_fd])
                        pc, d_pc = pj.next()
                        fw.op("pe", lambda e: e.matmul(pc[:, 0:cb], lhsT=w3[:, col0:col0 + 128], rhs=h2[:, 0:cb], start=True, stop=True),
                              reads=[d_w3, d_h2], writes=[d_pc])
                        pd, d_pd = pj.next()
                        fw.op("pe", lambda e: e.matmul(pd[:, 0:cb], lhsT=ones[0:1, 0:128], rhs=z[0:1, 0:cb], start=True, stop=True),
                              reads=[d_ones, d_z], writes=[d_pd])
                        dt_, d_dt = dec.next()
                        fw.op("act", lambda e: e.activation(out=dt_[:, 0:cb], in_=pd[:, 0:cb], func=AF.Exp, scale=nd[:, cg:cg + 1]),
                              reads=[d_pd, d_nd], writes=[d_dt])
                        fw.op("dve", lambda e: e.tensor_tensor(out=hdt[:, b * cb:(b + 1) * cb], in0=dt_[:, 0:cb], in1=pc[:, 0:cb], op=ALU.mult),
                              reads=[d_dt, d_pc], accs=[d_hd])
                        lo, hi = b * cb, (b + 1) * cb
                        if dr == 1:
                            if rev and b == ncb - 1:
                                hi -= 1
                            if (not rev) and b == 0:
                                lo += 1
                        fw.op("dve", lambda e: e.tensor_reduce(out=asum[:, dr * 32 + b:dr * 32 + b + 1], in_=hdt[:, lo:hi],
                                                               axis=AX.X, op=ALU.add, apply_absolute_value=True),
                              reads=[d_hd], accs=[d_as])
                fw.op("dve", lambda e: e.tensor_reduce(out=tot[:, 0:1], in_=asum[:], axis=AX.X, op=ALU.add), reads=[d_as], writes=[d_tot])
                fw.op("dve", lambda e: e.reciprocal(out=tot[:, 1:2], in_=tot[:, 0:1]), reads=[d_tot], accs=[d_tot])
                hf, d_hf = hd[0]
                hb_, d_hb = hd[1]
                bcol = hyb[:, order * 4 + cg:order * 4 + cg + 1]
                if order == 0:
                    fw.op("dve", lambda e: e.memset(kf[:, 0:1], 0.0), accs=[d_kf])
                    fw.op("act", lambda e: e.activation(out=kf[:, 1:L], in_=hb_[:, 0:L - 1], func=AF.Copy, scale=tot[:, 1:2]),
                          reads=[d_hb, d_tot], accs=[d_kf])
                    fw.op("act", lambda e: e.activation(out=kf[:, L + 1:2 * L], in_=hf[:, 1:L], func=AF.Copy, scale=tot[:, 1:2]),
                          reads=[d_hf, d_tot], accs=[d_kf])
                    fw.op("dve", lambda e: e.scalar_tensor_tensor(out=kf[:, L:L + 1], in0=hf[:, 0:1], scalar=tot[:, 1:2], in1=bcol,
                                                                  op0=ALU.mult, op1=ALU.add), reads=[d_hf, d_tot, d_hyb], accs=[d_kf])
                else:
                    fw.op("act", lambda e: e.activation(out=kf[:, 0:L - 1], in_=hf[:, 0:L - 1], func=AF.Copy, scale=tot[:, 1:2]),
                          reads=[d_hf, d_tot], accs=[d_kf])
                    fw.op("dve", lambda e: e.scalar_tensor_tensor(out=kf[:, L - 1:L], in0=hf[:, L - 1:L], scalar=tot[:, 1:2], in1=bcol,
                                                                  op0=ALU.mult, op1=ALU.add), reads=[d_hf, d_tot, d_hyb], accs=[d_kf])
                    fw.op("act", lambda e: e.activation(out=kf[:, L:2 * L - 1], in_=hb_[:, 1:L], func=AF.Copy, scale=tot[:, 1:2]),
                          reads=[d_hb, d_tot], accs=[d_kf])
                    fw.op("dve", lambda e: e.memset(kf[:, 2 * L - 1:2 * L], 0.0), accs=[d_kf])
                fw.dma("sp", Fdst[order * 512 + cg * 128:order * 512 + (cg + 1) * 128, 0:2 * L], kf[:, 0:2 * L], reads=[d_kf], track=d_kf)
    phase_end(fw, es)


def toep(fw, p, F, row, flen, mode, nbd, nb, rhs, d_rhs, ps, d_ps, msk_ring):
    C = flen // 2
    if mode == "A":
        base = lambda d: C - 127 + 128 * d
    else:
        base = lambda d: C - 128 - 128 * d
    bmin = min(base(-nbd), base(nbd))
    bmax = max(base(-nbd), base(nbd))
    Wd = ((bmax - bmin + 128 + 127) // 128) * 128
    mk, d_mk = msk_ring.next()
    src = bass.AP(tensor=F, offset=row * (flen + 256) + bmin, ap=[[1, 128], [1, Wd]])
    fw.dma("sp", mk[:, 0:Wd], src, writes=[d_mk])
    order = [0] + [d for d in range(-nbd, nbd + 1) if d != 0]
    for k, d in enumerate(order):
        i0, i1 = max(0, d), min(nb, nb + d)
        if i1 <= i0:
            continue
        b0 = base(d) - bmin
        fw.op("pe", lambda e: e.matmul(ps[:, i0:i1], lhsT=mk[:, b0:b0 + 128], rhs=rhs[:, i0 - d:i1 - d],
                                       start=(k == 0), stop=(k == len(order) - 1)), reads=[d_mk, d_rhs], accs=[d_ps])


def emit_B(p, l, do_ctx):
    fw, nc = p.fw, p.nc
    W = p.W[l]
    es = phase_begin(fw)
    pj = p.pjB
    bsrep, d_bs = load_rep(fw, "bsrep", W["b_short"].ap(), 1536)
    seqs = [(0, p.NL, p.FL, 2 * p.S, p.invc_lat)]
    if do_ctx:
        seqs.append((p.NL, 2, p.FC, 2 * CTX, p.invc_ctx))
    NBM = p.NL
    mskL = Ring(fw, "mskL", 2, [128, 2 * p.S], BF16)
    mskS = Ring(fw, "mskS", 4, [128, 512], BF16)
    stg = Ring(fw, "bstg", 2, [128, NBM * 128], BF16)
    U = [(fw.sb("U%d" % i, [128, 128 * NBM], BF16), Dep("U%d" % i)) for i in range(3)]
    Yg = Ring(fw, "Yg", 2, [128, NBM * 128], BF16)
    sm = Ring(fw, "bsm", 8, [128, NBM], F32)
    smb = Ring(fw, "bsmb", 8, [128, NBM], BF16)
    for (t0, nb, FLg, flen, invc_d) in seqs:
        invc = fw.sb("invc%d" % t0, [128, 4 * nb], F32)
        d_invc = Dep("invc")
        fw.dma("sp", invc[:], invc_d.ap(), writes=[d_invc])

        def load_stream(dst, d_dst, src_t, rowlen, col0):
            st, d_st = stg.next()
            for j0 in range(0, nb, 8):
                jn = min(8, nb - j0)
                src = bass.AP(tensor=src_t, offset=(t0 + j0) * 128 * rowlen + col0, ap=[[rowlen, 128], [128 * rowlen, jn], [1, 128]])
                fw.dma("sp", st[:, j0 * 128:(j0 + jn) * 128].rearrange("p (j c) -> p j c", c=128), src, accs=[d_st])
            cp(fw, "pool", dst[:, 0:128 * nb].rearrange("p (c j) -> p c j", j=nb),
               st[:, 0:nb * 128].rearrange("p (j c) -> p c j", c=128), [d_st], writes=[d_dst])

        for cg in range(4):
            load_stream(U[0][0], U[0][1], p.Y, INW, H0C + cg * 128)
            load_stream(U[1][0], U[1][1], p.YR, 1024, 512 + cg * 128)
            load_stream(U[2][0], U[2][1], p.Y, INW, H0C + 1024 + cg * 128)
            yg, d_yg = Yg.next()
            svs = {}
            zs = {}

            def S0(c):
                ch = cg * 128 + c
                uv = U[0][0][:, c * nb:(c + 1) * nb]
                p1, d_p1 = pj.next()
                toep(fw, p, p.FS, 0 * 512 + ch, 512, "B", 1, nb, uv, U[0][1], p1, d_p1, mskS)
                sv, d_sv = smb.next()
                fw.op("act", lambda e: e.activation(out=sv[:, 0:nb], in_=p1[:, 0:nb], func=AF.Identity, bias=bsrep[:, ch:ch + 1]),
                      reads=[d_p1, d_bs], writes=[d_sv])
                svs[c] = (sv, d_sv)

            def S1(c):
                ch = cg * 128 + c
                ux1 = U[1][0][:, c * nb:(c + 1) * nb]
                sv, d_sv = svs.pop(c)
                p2, d_p2 = pj.next()
                toep(fw, p, FLg, ch, flen, "A", nb - 1, nb, sv, d_sv, p2, d_p2, mskL)
                p3, d_p3 = pj.next()
                toep(fw, p, p.FS, 1 * 512 + ch, 512, "A", 1, nb, ux1, U[1][1], p3, d_p3, mskS)
                s1, d_s1 = sm.next()
                fw.op("act", lambda e: e.activation(out=s1[:, 0:nb], in_=p3[:, 0:nb], func=AF.Identity, bias=bsrep[:, 512 + ch:512 + ch + 1]),
                      reads=[d_p3, d_bs], writes=[d_s1])
                z, d_z = smb.next()
                fw.op("dve", lambda e: e.tensor_tensor(out=z[:, 0:nb], in0=s1[:, 0:nb], in1=p2[:, 0:nb], op=ALU.mult),
                      reads=[d_s1, d_p2], writes=[d_z])
                zs[c] = (z, d_z)

            def S2(c):
                ch = cg * 128 + c
                ux2 = U[2][0][:, c * nb:(c + 1) * nb]
                z, d_z = zs.pop(c)
                p4, d_p4 = pj.next()
                toep(fw, p, FLg, 512 + ch, flen, "B", nb - 1, nb, z, d_z, p4, d_p4, mskL)
                p5, d_p5 = pj.next()
                toep(fw, p, p.FS, 2 * 512 + ch, 512, "B", 1, nb, ux2, U[2][1], p5, d_p5, mskS)
                s2, d_s2 = sm.next()
                fw.op("act", lambda e: e.activation(out=s2[:, 0:nb], in_=p5[:, 0:nb], func=AF.Identity, bias=bsrep[:, 1024 + ch:1024 + ch + 1]),
                      reads=[d_p5, d_bs], writes=[d_s2])
                yv = sbap(yg, NBM * 128, c, [[128, nb]])
                fw.op("dve", lambda e: e.tensor_tensor(out=yv, in0=s2[:, 0:nb], in1=p4[:, 0:nb], op=ALU.mult),
                      reads=[d_s2, d_p4], accs=[d_yg])

            for i in range(-1, 129):
                if 0 <= i + 1 < 128:
                    S0(i + 1)
                if 0 <= i < 128:
                    S1(i)
                if 0 <= i - 1 < 128:
                    S2(i - 1)
            for j0 in range(0, nb, 8):
                jn = min(8, nb - j0)
                dst = bass.AP(tensor=p.BO, offset=(t0 + j0) * 128 * 1024 + 512 + cg * 128, ap=[[1024, 128], [128 * 1024, jn], [1, 128]])
                fw.dma("pool", dst, yg[:, j0 * 128:(j0 + jn) * 128].rearrange("p (j c) -> p j c", c=128), reads=[d_yg], track=d_yg)
            load_stream(U[0][0], U[0][1], p.YR, 1024, cg * 128)
            load_stream(U[1][0], U[1][1], p.Y, INW, 1536 + cg * 128)
            yg, d_yg = Yg.next()
            for c in range(128):
                ur = U[0][0][:, c * nb:(c + 1) * nb]
                un = U[1][0][:, c * nb:(c + 1) * nb]
                p1, d_p1 = pj.next()
                toep(fw, p, p.FP, cg, 512, "A", 1, nb, ur, U[0][1], p1, d_p1, mskS)
                s1, d_s1 = sm.next()
                fw.op("dve", lambda e: e.tensor_tensor(out=s1[:, 0:nb], in0=p1[:, 0:nb], in1=invc[:, cg * nb:(cg + 1) * nb], op=ALU.mult),
                      reads=[d_p1, d_invc], writes=[d_s1])
                yv = sbap(yg, NBM * 128, c, [[128, nb]])
                fw.op("dve", lambda e: e.tensor_tensor(out=yv, in0=s1[:, 0:nb], in1=un, op=ALU.subtract),
                      reads=[d_s1, U[1][1]], accs=[d_yg])
            for j0 in range(0, nb, 8):
                jn = min(8, nb - j0)
                dst = bass.AP(tensor=p.BO, offset=(t0 + j0) * 128 * 1024 + cg * 128, ap=[[1024, 128], [128 * 1024, jn], [1, 128]])
                fw.dma("pool", dst, yg[:, j0 * 128:(j0 + jn) * 128].rearrange("p (j c) -> p j c", c=128), reads=[d_yg], track=d_yg)
    phase_end(fw, es)


def emit_T(p, l, do_ctx):
    fw, nc = p.fw, p.nc
    W = p.W[l]
    es = phase_begin(fw)
    pj = p.pj
    NT = p.NT
    identb, d_idb = p.identb
    lq, d_lq = load_rep(fw, "lq", W["lamqk"].ap(), 256)
    lam = fw.sb("lam", [128, 8], F32)
    d_lam = Dep("lam")
    lt = fw.sb("lqt", [128, 128], F32)
    d_lt = Dep("lqt")
    for i in range(2):
        fw.op("dve", lambda e: e.tensor_tensor(out=lt[:, i * 64:(i + 1) * 64], in0=lq[:, i * 128:i * 128 + 64], in1=lq[:, i * 128 + 64:(i + 1) * 128],
                                               op=ALU.mult), reads=[d_lq], accs=[d_lt])
        fw.op("dve", lambda e: e.tensor_reduce(out=lam[:, i:i + 1], in_=lt[:, i * 64:(i + 1) * 64], axis=AX.X, op=ALU.add),
              reads=[d_lt], accs=[d_lam])
    fw.op("act", lambda e: e.activation(out=lam[:, 2:4], in_=lam[:, 0:2], func=AF.Exp), reads=[d_lam], accs=[d_lam])
    fw.op("dve", lambda e: e.tensor_tensor(out=lam[:, 4:5], in0=lam[:, 2:3], in1=lam[:, 3:4], op=ALU.subtract), reads=[d_lam], accs=[d_lam])
    fw.op("dve", lambda e: e.tensor_scalar(out=lam[:, 5:6], in0=lam[:, 4:5], scalar1=LAM_INIT[l], scalar2=-1.0, op0=ALU.add, op1=ALU.mult),
          reads=[d_lam], accs=[d_lam])
    sg, d_sg = load_rep(fw, "subg", W["subg"].ap(), 128)
    fw.op("dve", lambda e: e.tensor_scalar(out=sg[:], in0=sg[:], scalar1=1.0 - LAM_INIT[l], scalar2=None, op0=ALU.mult),
          reads=[d_sg], writes=[d_sg])
    tok = Ring(fw, "atok", 2, [128, NT * 128], BF16)
    kT = fw.sb("kT", [128, NT * 128], BF16)
    d_kT = Dep("kT")
    qT = fw.sb("qT", [128, NT * 128], BF16)
    d_qT = Dep("qT")
    V1 = fw.sb("V1", [128, NT * 129], BF16)
    d_V1 = Dep("V1")
    ET = Ring(fw, "ET", 3, [128, 512], BF16)
    osb = Ring(fw, "osb", 2, [128, 128], F32)
    aob = Ring(fw, "aob", 2, [128, 128], BF16)
    st = Ring(fw, "ast", 2, [128, 8], F32)
    sq = fw.sb("asq", [128, 128], F32)
    d_sq = Dep("asq")
    OP = p.OP
    for h in range(4):
        for which, dst, d_dst in ((0, qT, d_qT), (1, kT, d_kT)):
            tk, d_tk = tok.next()
            for j0 in range(0, NT, 8):
                jn = min(8, NT - j0)
                src = bass.AP(tensor=p.Y, offset=j0 * 128 * INW + which * 512 + h * 128, ap=[[INW, 128], [128 * INW, jn], [1, 128]])
                fw.dma("sp", tk[:, j0 * 128:(j0 + jn) * 128].rearrange("p (j c) -> p j c", c=128), src, accs=[d_tk])
            for t4 in range(0, NT, 4):
                n = min(4, NT - t4)
                pt, d_pt = p.tpb_ps.next()
                for j in range(n):
                    fw.op("pe", lambda e: e.transpose(pt[:, j * 128:(j + 1) * 128], tk[:, (t4 + j) * 128:(t4 + j + 1) * 128], identb[:]),
                          reads=[d_tk, d_idb], accs=[d_pt])
                cp(fw, "dve", dst[:, t4 * 128:(t4 + n) * 128], pt[:, 0:n * 128], [d_pt], accs=[d_dst])
        for j0 in range(0, NT, 8):
            jn = min(8, NT - j0)
            srcv = bass.AP(tensor=p.Y, offset=j0 * 128 * INW + 1024 + h * 128, ap=[[INW, 128], [128 * INW, jn], [1, 128]])
            fw.dma("sp", sbap(V1, NT * 129, j0 * 129, [[129, jn], [1, 128]]), srcv, accs=[d_V1])
        fw.op("dve", lambda e: e.memset(sbap(V1, NT * 129, 128, [[129, NT]]), 1.0), accs=[d_V1])
        qsets = [(0, p.NL, list(range(NT)))]
        if do_ctx:
            qsets.append((p.NL, 2, [p.NL, p.NL + 1]))
        for (qt0, nqt, ktiles) in qsets:
            for qb0 in range(0, nqt, 2):
                nq = min(2, nqt - qb0)
                qc0 = (qt0 + qb0) * 128
                for ki, kt in enumerate(ktiles):
                    for m in range(2):
                        sp_, d_sp = pj.next()
                        fw.op("pe", lambda e: e.matmul(sp_[:, 0:nq * 128], lhsT=kT[m * 64:(m + 1) * 64, kt * 128:(kt + 1) * 128],
                                                       rhs=qT[m * 64:(m + 1) * 64, qc0:qc0 + nq * 128], start=True, stop=True),
                              reads=[d_kT, d_qT], writes=[d_sp])
                        et, d_et = ET.next()
                        fw.op("act", lambda e: e.activation(out=et[:, 0:nq * 128], in_=sp_[:, 0:nq * 128], func=AF.Exp, scale=0.125),
                              reads=[d_sp], writes=[d_et])
                        for qs in range(nq):
                            ot, d_ot = OP[m * 2 + qs]
                            oc = 0
                            fw.op("pe", lambda e: e.matmul(ot[:, oc:oc + 129], lhsT=et[:, qs * 128:(qs + 1) * 128],
                                                           rhs=V1[:, kt * 129:(kt + 1) * 129], start=(ki == 0), stop=(ki == len(ktiles) - 1)),
                                  reads=[d_et, d_V1], accs=[d_ot])
                for qs in range(nq):
                    s_, d_s = st.next()
                    o0t, d_o0 = OP[qs]
                    o0c = 0
                    o1t, d_o1 = OP[2 + qs]
                    o1c = 0
                    fw.op("dve", lambda e: e.reciprocal(out=s_[:, 0:1], in_=o0t[:, o0c + 128:o0c + 129]), reads=[d_o0], accs=[d_s])
                    fw.op("dve", lambda e: e.reciprocal(out=s_[:, 1:2], in_=o1t[:, o1c + 128:o1c + 129]), reads=[d_o1], accs=[d_s])
                    fw.op("dve", lambda e: e.tensor_tensor(out=s_[:, 2:3], in0=s_[:, 1:2], in1=lam[:, 5:6], op=ALU.mult), reads=[d_s, d_lam], accs=[d_s])
                    o, d_o = osb.next()
                    fw.op("dve", lambda e: e.tensor_scalar(out=o[:], in0=o0t[:, o0c:o0c + 128], scalar1=s_[:, 0:1], scalar2=None, op0=ALU.mult),
                          reads=[d_o0, d_s], writes=[d_o])
                    fw.op("dve", lambda e: e.scalar_tensor_tensor(out=o[:], in0=o1t[:, o1c:o1c + 128], scalar=s_[:, 2:3], in1=o[:],
                                                                  op0=ALU.mult, op1=ALU.add), reads=[d_o1, d_s, d_o], writes=[d_o])
                    fw.op("act", lambda e: e.activation(out=sq[:], in_=o[:], func=AF.Square, accum_out=s_[:, 3:4]), reads=[d_o], writes=[d_sq], accs=[d_s])
                    fw.op("dve", lambda e: e.tensor_scalar(out=s_[:, 4:5], in0=s_[:, 3:4], scalar1=1.0 / 128, scalar2=EPS, op0=ALU.mult, op1=ALU.add),
                          reads=[d_s], accs=[d_s])
                    fw.op("act", lambda e: e.activation(out=s_[:, 4:5], in_=s_[:, 4:5], func=AF.Sqrt), reads=[d_s], accs=[d_s])
                    fw.op("dve", lambda e: e.reciprocal(out=s_[:, 4:5], in_=s_[:, 4:5]), reads=[d_s], accs=[d_s])
                    ao, d_ao = aob.next()
                    fw.op("dve", lambda e: e.scalar_tensor_tensor(out=ao[:], in0=o[:], scalar=s_[:, 4:5], in1=sg[:], op0=ALU.mult, op1=ALU.mult),
                          reads=[d_o, d_s, d_sg], writes=[d_ao])
                    row0 = (qt0 + qb0 + qs) * 128
                    fw.dma("pool", p.AO[row0:row0 + 128, h * 128:(h + 1) * 128], ao[:], reads=[d_ao], track=d_ao)
    phase_end(fw, es)


def transpose_to(fw, p, src, d_src, ncol, dst, d_dst, ident_t, use_dve=True):
    idt, d_idt = ident_t
    nk = ncol // 128
    for k0 in range(0, nk, 4):
        n = min(4, nk - k0)
        pt, d_pt = p.tpb_ps.next()
        for j in range(n):
            fw.op("pe", lambda e: e.transpose(pt[:, j * 128:(j + 1) * 128], src[:, (k0 + j) * 128:(k0 + j + 1) * 128], idt[:]),
                  reads=[d_src, d_idt], accs=[d_pt])
        cp(fw, "dve" if use_dve else "act", dst[:, k0 * 128:(k0 + n) * 128], pt[:, 0:n * 128], [d_pt], accs=[d_dst])


def emit_C1(p, l, do_ctx):
    fw, nc = p.fw, p.nc
    W = p.W[l]
    es = phase_begin(fw)
    pj = p.pjbig
    es2 = ExitStack()
    fw.es = es
    wao = fw.sb("wao", [128, 4 * 1024], BF16); d_wao = Dep("wao")
    wpo = fw.sb("wpo", [128, 4 * 1024], BF16); d_wpo = Dep("wpo")
    why = fw.sb("why", [128, 4 * 1024], BF16); d_why = Dep("why")
    wout = fw.sb("wout", [128, 8 * 1024], BF16); d_wout = Dep("wout")
    wpl = fw.sb("wpl", [128, 4 * 128], BF16); d_wpl = Dep("wpl")
    GT = [(fw.sb("gate1_%d" % r, [128, D], F32), Dep("gate1_%d" % r)) for r in range(2)]
    psc = fw.sb("psc", [128, 4], F32); d_psc = Dep("psc")
    fw.dma("sp", psc[:], W["psc"].ap(), writes=[d_psc])
    fw.es = es2
    stg = Ring(fw, "wstg", 2, [128, 2048], F32)
    load_w_bf16(fw, wao, d_wao, W["w_ao"], 512, 1024, stg)
    load_w_bf16(fw, wpo, d_wpo, W["w_po"], 512, 1024, stg)
    load_w_bf16(fw, why, d_why, W["w_hy"], 512, 1024, stg)
    load_w_bf16(fw, wout, d_wout, W["w_out"], 1024, 1024, stg)
    load_w_bf16(fw, wpl, d_wpl, W["w_pool"], 512, 128, stg)
    emit_mod(fw, p.cvec, W["w_mod"], W["b_mod"], [4, 5], pj, GT, p.ones, p.ident)
    fw.barrier()
    es2.close()
    fw.es = es
    aoR = Ring(fw, "c_ao", 2, [128, 512], BF16)
    boR = Ring(fw, "c_bo", 2, [128, 1024], BF16)
    gR = Ring(fw, "c_g", 2, [128, 3072], BF16)
    hR = Ring(fw, "c_h", 2, [128, D], F32)
    aoT = fw.sb("aoT", [128, 512], BF16); d_aoT = Dep("aoT")
    pT = fw.sb("pT", [128, 512], BF16); d_pT = Dep("pT")
    yT = fw.sb("yT", [128, 512], BF16); d_yT = Dep("yT")
    mT = fw.sb("mT", [128, 512], BF16); d_mT = Dep("mT")
    mg = fw.sb("mg", [128, D], F32); d_mg = Dep("mg")
    mgb = fw.sb("mgb", [128, D], BF16); d_mgb = Dep("mgb")
    mgT = fw.sb("mgT", [128, D], BF16); d_mgT = Dep("mgT")
    tmp = Ring(fw, "c_tmp", 2, [128, 512], F32)
    hm = Ring(fw, "c_hm", 2, [128, D], F32)
    ntl = p.NT if do_ctx else p.NL
    for t in range(ntl):
        r = 1 if t >= p.NL else 0
        ao, d_ao = aoR.next()
        fw.dma("sp", ao[:], p.AO[t * 128:(t + 1) * 128, :], writes=[d_ao])
        bo, d_bo = boR.next()
        fw.dma("sp", bo[:], p.BO[t * 128:(t + 1) * 128, :], writes=[d_bo])
        g, d_g = gR.next()
        fw.dma("sp", g[:], p.Y[t * 128:(t + 1) * 128, G0C:INW], writes=[d_g])
        ht, d_ht = hR.next()
        fw.dma("sp", ht[:], p.h_src(l, t), writes=[d_ht])
        transpose_to(fw, p, ao, d_ao, 512, aoT, d_aoT, p.identb)
        transpose_to(fw, p, bo[:, 0:512], d_bo, 512, pT, d_pT, p.identb, use_dve=False)
        transpose_to(fw, p, bo[:, 512:1024], d_bo, 512, yT, d_yT, p.Jb)
        for gi in range(4):
            pt, d_pt = pj.next()
            fw.op("pe", lambda e: e.matmul(pt[:, 0:128], lhsT=wpl[:, gi * 128:(gi + 1) * 128], rhs=pT[:, gi * 128:(gi + 1) * 128],
                                           start=True, stop=True), reads=[d_wpl, d_pT], writes=[d_pt])
            fw.op("act", lambda e: e.activation(out=mT[:, gi * 128:(gi + 1) * 128], in_=pt[:, 0:128], func=AF.Copy, scale=psc[:, gi:gi + 1]),
                  reads=[d_pt, d_psc], accs=[d_mT])
        for nb in range(2):
            for bi, (xT, d_xT, wt, d_wt) in enumerate(((aoT, d_aoT, wao, d_wao), (mT, d_mT, wpo, d_wpo), (yT, d_yT, why, d_why))):
                pt, d_pt = pj.next()
                for kc in range(4):
                    fw.op("pe", lambda e: e.matmul(pt[:], lhsT=xT[:, kc * 128:(kc + 1) * 128],
                                                   rhs=wt[:, kc * 1024 + nb * 512:kc * 1024 + (nb + 1) * 512],
                                                   start=(kc == 0), stop=(kc == 3)), reads=[d_xT, d_wt], accs=[d_pt])
                gs = g[:, bi * 1024 + nb * 512:bi * 1024 + (nb + 1) * 512]
                if bi == 0:
                    fw.op("dve", lambda e: e.tensor_tensor(out=mg[:, nb * 512:(nb + 1) * 512], in0=pt[:], in1=gs, op=ALU.mult),
                          reads=[d_pt, d_g], accs=[d_mg])
                else:
                    tt, d_tt = tmp.next()
                    fw.op("dve", lambda e: e.tensor_tensor(out=tt[:], in0=pt[:], in1=gs, op=ALU.mult), reads=[d_pt, d_g], writes=[d_tt])
                    fw.op("dve", lambda e: e.tensor_tensor(out=mg[:, nb * 512:(nb + 1) * 512], in0=mg[:, nb * 512:(nb + 1) * 512], in1=tt[:], op=ALU.add),
                          reads=[d_tt, d_mg], accs=[d_mg])
        cp(fw, "act", mgb[:], mg[:], [d_mg], writes=[d_mgb])
        transpose_to(fw, p, mgb, d_mgb, 1024, mgT, d_mgT, p.identb)
        hmt, d_hm = hm.next()
        gt1, d_gt1 = GT[r]
        for nb in range(2):
            pt, d_pt = pj.next()
            for kc in range(8):
                fw.op("pe", lambda e: e.matmul(pt[:], lhsT=mgT[:, kc * 128:(kc + 1) * 128],
                                               rhs=wout[:, kc * 1024 + nb * 512:kc * 1024 + (nb + 1) * 512],
                                               start=(kc == 0), stop=(kc == 7)), reads=[d_mgT, d_wout], accs=[d_pt])
            tt, d_tt = tmp.next()
            fw.op("dve", lambda e: e.tensor_tensor(out=tt[:], in0=pt[:], in1=gt1[:, nb * 512:(nb + 1) * 512], op=ALU.mult),
                  reads=[d_pt, d_gt1], writes=[d_tt])
            fw.op("dve", lambda e: e.tensor_tensor(out=hmt[:, nb * 512:(nb + 1) * 512], in0=ht[:, nb * 512:(nb + 1) * 512], in1=tt[:], op=ALU.add),
                  reads=[d_tt, d_ht], accs=[d_hm])
        fw.dma("pool", p.HM[t * 128:(t + 1) * 128, :], hmt[:], reads=[d_hm], track=d_hm)
    phase_end(fw, es)


def emit_C2(p, l, do_ctx, last):
    fw, nc = p.fw, p.nc
    W = p.W[l]
    es = phase_begin(fw)
    pj = p.pjbig
    ident, d_id = p.ident
    G2 = [(fw.sb("G2_%d" % r, [128, D], F32), Dep("G2_%d" % r)) for r in range(2)]
    SH2 = [(fw.sb("SH2_%d" % r, [128, D], F32), Dep("SH2_%d" % r)) for r in range(2)]
    GA2 = [(fw.sb("GA2_%d" % r, [128, D], F32), Dep("GA2_%d" % r)) for r in range(2)]
    es2 = ExitStack()
    fw.es = es2
    MD = [(fw.sb("md2_%d" % r, [128, 3 * D], F32), Dep("md2_%d" % r)) for r in range(2)]
    emit_mod(fw, p.cvec, W["w_mod"], W["b_mod"], [6, 7, 8, 9, 10, 11], pj, MD, p.ones, p.ident)
    g2t, d_g2 = load_rep(fw, "g2t", W["g2"].ap(), D)
    for r in range(2):
        gt, d_gt = G2[r]
        mt, d_mt = MD[r]
        fw.op("dve", lambda e: e.scalar_tensor_tensor(out=gt[:], in0=mt[:, 1024:2048], scalar=1.0, in1=g2t[:],
                                                      op0=ALU.add, op1=ALU.mult), reads=[d_mt, d_g2], writes=[d_gt])
        cp(fw, "dve", SH2[r][0][:], mt[:, 0:1024], [d_mt], writes=[SH2[r][1]])
        cp(fw, "dve", GA2[r][0][:], mt[:, 2048:3072], [d_mt], writes=[GA2[r][1]])
    fw.barrier()
    es2.close()
    fw.es = es
    wfi = fw.sb("wfi", [128, 8 * 2 * FFN], BF16); d_wfi = Dep("wfi")
    wfo = fw.sb("wfo", [128, 22 * 1024], BF16); d_wfo = Dep("wfo")
    es2 = ExitStack()
    fw.es = es2
    stg = Ring(fw, "wstg", 2, [128, 2048], F32)
    load_w_bf16(fw, wfi, d_wfi, W["w_fi"], 1024, 2 * FFN, stg)
    load_w_bf16(fw, wfo, d_wfo, W["w_fo"], FFN, 1024, stg)
    fw.barrier()
    es2.close()
    fw.es = es
    fgt = None
    if last:
        fgt, d_fg = load_rep(fw, "fgt", p.final_g.ap(), D)
    hR = Ring(fw, "f_h", 2, [128, D], F32)
    a2 = fw.sb("f_a2", [128, D], F32); d_a2 = Dep("f_a2")
    a2T = fw.sb("f_a2T", [128, D], BF16); d_a2T = Dep("f_a2T")
    hT = fw.sb("f_hT", [128, 22 * 128], BF16); d_hT = Dep("f_hT")
    sg = Ring(fw, "f_sg", 2, [128, 128], F32)
    sq = fw.sb("f_sq", [128, D], F32); d_sq = Dep("f_sq")
    stat = Ring(fw, "f_st", 2, [128, 4], F32)
    tmp = Ring(fw, "f_tmp", 2, [128, 512], F32)
    hn = Ring(fw, "f_hn", 1, [128, D], F32)
    on = Ring(fw, "f_on", 1, [128, D], F32)
    ntl = p.NT if do_ctx else p.NL
    for t in range(ntl):
        r = 1 if t >= p.NL else 0
        ht, d_ht = hR.next()
        fw.dma("sp", ht[:], p.HM[t * 128:(t + 1) * 128, :], writes=[d_ht])
        st, d_st = stat.next()
        fw.op("act", lambda e: e.activation(out=sq[:], in_=ht[:], func=AF.Square, accum_out=st[:, 0:1]), reads=[d_ht], writes=[d_sq, d_st])
        fw.op("dve", lambda e: e.tensor_scalar(out=st[:, 1:2], in0=st[:, 0:1], scalar1=1.0 / D, scalar2=EPS, op0=ALU.mult, op1=ALU.add),
              reads=[d_st], accs=[d_st])
        fw.op("act", lambda e: e.activation(out=st[:, 1:2], in_=st[:, 1:2], func=AF.Sqrt), reads=[d_st], accs=[d_st])
        fw.op("dve", lambda e: e.reciprocal(out=st[:, 1:2], in_=st[:, 1:2]), reads=[d_st], accs=[d_st])
        gt, d_gt = G2[r]
        sh2, d_sh2 = SH2[r]
        ga2, d_ga2 = GA2[r]
        fw.op("dve", lambda e: e.scalar_tensor_tensor(out=a2[:], in0=ht[:], scalar=st[:, 1:2], in1=gt[:], op0=ALU.mult, op1=ALU.mult),
              reads=[d_ht, d_st, d_gt], writes=[d_a2])
        fw.op("dve", lambda e: e.tensor_add(out=a2[:], in0=a2[:], in1=sh2[:]), reads=[d_a2, d_sh2], writes=[d_a2])
        for half in range(2):
            pt, d_pt = p.tp.next()
            for j in range(4):
                kc = half * 4 + j
                fw.op("pe", lambda e: e.transpose(pt[:, j * 128:(j + 1) * 128], a2[:, kc * 128:(kc + 1) * 128], ident[:]),
                      reads=[d_a2, d_id], accs=[d_pt])
            cp(fw, "dve" if half == 0 else "act", a2T[:, half * 512:(half + 1) * 512], pt[:], [d_pt], accs=[d_a2T])
        for hc in range(22):
            pg, d_pg = pj.next()
            for which in range(2):
                c0 = which * FFN + hc * 128
                for kc in range(8):
                    fw.op("pe", lambda e: e.matmul(pg[:, which * 128:(which + 1) * 128], lhsT=wfi[:, kc * 2 * FFN + c0:kc * 2 * FFN + c0 + 128],
                                                   rhs=a2T[:, kc * 128:(kc + 1) * 128], start=(kc == 0), stop=(kc == 7)),
                          reads=[d_wfi, d_a2T], accs=[d_pg])
            s_, d_s = sg.next()
            fw.op("act", lambda e: e.activation(out=s_[:], in_=pg[:, 0:128], func=AF.Silu), reads=[d_pg], writes=[d_s])
            fw.op("dve", lambda e: e.tensor_tensor(out=hT[:, hc * 128:(hc + 1) * 128], in0=s_[:], in1=pg[:, 128:256], op=ALU.mult),
                  reads=[d_s, d_pg], accs=[d_hT])
        hnt, d_hn = hn.next()
        for nb in range(2):
            pt, d_pt = pj.next()
            for kc in range(22):
                fw.op("pe", lambda e: e.matmul(pt[:], lhsT=hT[:, kc * 128:(kc + 1) * 128],
                                               rhs=wfo[:, kc * 1024 + nb * 512:kc * 1024 + (nb + 1) * 512],
                                               start=(kc == 0), stop=(kc == 21)), reads=[d_hT, d_wfo], accs=[d_pt])
            tt, d_tt = tmp.next()
            fw.op("dve", lambda e: e.tensor_tensor(out=tt[:], in0=pt[:], in1=ga2[:, nb * 512:(nb + 1) * 512], op=ALU.mult),
                  reads=[d_pt, d_ga2], writes=[d_tt])
            fw.op("dve", lambda e: e.tensor_tensor(out=hnt[:, nb * 512:(nb + 1) * 512], in0=ht[:, nb * 512:(nb + 1) * 512], in1=tt[:], op=ALU.add),
                  reads=[d_tt, d_ht], accs=[d_hn])
        if not last:
            fw.dma("pool", p.HS[t * 128:(t + 1) * 128, :], hnt[:], reads=[d_hn], track=d_hn)
        else:
            st2, d_st2 = stat.next()
            fw.op("act", lambda e: e.activation(out=sq[:], in_=hnt[:], func=AF.Square, accum_out=st2[:, 0:1]), reads=[d_hn], writes=[d_sq, d_st2])
            fw.op("dve", lambda e: e.tensor_scalar(out=st2[:, 1:2], in0=st2[:, 0:1], scalar1=1.0 / D, scalar2=EPS, op0=ALU.mult, op1=ALU.add),
                  reads=[d_st2], accs=[d_st2])
            fw.op("act", lambda e: e.activation(out=st2[:, 1:2], in_=st2[:, 1:2], func=AF.Sqrt), reads=[d_st2], accs=[d_st2])
            fw.op("dve", lambda e: e.reciprocal(out=st2[:, 1:2], in_=st2[:, 1:2]), reads=[d_st2], accs=[d_st2])
            ot, d_ot = on.next()
            fw.op("dve", lambda e: e.scalar_tensor_tensor(out=ot[:], in0=hnt[:], scalar=st2[:, 1:2], in1=fgt[:], op0=ALU.mult, op1=ALU.mult),
                  reads=[d_hn, d_st2, d_fg], writes=[d_ot])
            fw.dma("pool", p.out[t * 128:(t + 1) * 128, :], ot[:], reads=[d_ot], track=d_ot)
    phase_end(fw, es)


LAYER_KEYS = ["w_in", "w_mod", "b_mod", "g1", "g2", "lamqk", "subg", "w_ao", "w_pool", "psc", "w_po", "w_short", "b_short",
              "hf_w1", "hf_w2", "hf_w3", "hf_par", "hyb", "w_hy", "w_out", "w_fi", "w_fo"]
LAYER_SHAPES = {"w_in": [D, INW], "w_mod": [D, 6 * D], "b_mod": [1, 6 * D], "g1": [1, D], "g2": [1, D], "lamqk": [1, 256],
                "subg": [1, 128], "w_ao": [512, D], "w_pool": [512, 128], "psc": [128, 4], "w_po": [512, D], "w_short": [3, 1536],
                "b_short": [1, 1536], "hf_w1": [33, 64], "hf_w2": [64, 64], "hf_w3": [64, 2048], "hf_par": [64, 4], "hyb": [128, 8],
                "w_hy": [512, D], "w_out": [D, D], "w_fi": [D, 2 * FFN], "w_fo": [FFN, D]}


KSTOP = 12


def build_all(S):
    p = P()
    p.S = S
    p.NL = S // 128
    p.NT = p.NL + 2
    NTOK = p.NT * 128
    nc = bass.Bass("TRN2", target_bir_lowering=False)
    p.nc = nc
    x = nc.dram_tensor("x", [S, D], F32, kind="ExternalInput")
    cx = nc.dram_tensor("ctxin", [CTX, D], F32, kind="ExternalInput")
    p.cvec = nc.dram_tensor("cvec", [2, D], F32, kind="ExternalInput")
    p.W = []
    for l in range(2):
        p.W.append({k: nc.dram_tensor("%s_%d" % (k, l), LAYER_SHAPES[k], F32, kind="ExternalInput") for k in LAYER_KEYS})
    p.final_g = nc.dram_tensor("final_g", [1, D], F32, kind="ExternalInput")
    p.cs = nc.dram_tensor("cs", [NTOK, 64], F32, kind="ExternalInput")
    p.zt_lat = nc.dram_tensor("zt_lat", [2, 33, S], F32, kind="ExternalInput")
    p.zt_ctx = nc.dram_tensor("zt_ctx", [2, 33, CTX], F32, kind="ExternalInput")
    p.negdelta = nc.dram_tensor("negdelta", [128, 4], F32, kind="ExternalInput")
    p.invc_lat = nc.dram_tensor("invc_lat", [128, 4 * p.NL], F32, kind="ExternalInput")
    p.invc_ctx = nc.dram_tensor("invc_ctx", [128, 8], F32, kind="ExternalInput")
    p.FP = nc.dram_tensor("fpool", [4, 768], BF16, kind="ExternalInput")
    p.out = nc.dram_tensor("out", [S, D], F32, kind="ExternalOutput")
    p.Y = nc.dram_tensor("Ysc", [NTOK, INW], BF16)
    p.YR = nc.dram_tensor("YRsc", [NTOK, 1024], BF16)
    p.FS = nc.dram_tensor("FSsc", [1536, 768], BF16)
    p.FL = nc.dram_tensor("FLsc", [1024, 2 * S + 256], BF16)
    p.FC = nc.dram_tensor("FCsc", [1024, 2 * CTX + 256], BF16)
    p.BO = nc.dram_tensor("BOsc", [NTOK, 1024], BF16)
    p.AO = nc.dram_tensor("AOsc", [NTOK, 512], BF16)
    p.HM = nc.dram_tensor("HMsc", [NTOK, D], F32)
    p.HS = nc.dram_tensor("HSsc", [NTOK, D], F32)

    def h_src(l, t):
        if l == 0:
            if t < p.NL:
                return x[t * 128:(t + 1) * 128, :]
            return cx[(t - p.NL) * 128:(t - p.NL + 1) * 128, :]
        return p.HS[t * 128:(t + 1) * 128, :]
    p.h_src = h_src
    fw = FW(nc)
    p.fw = fw
    p.ident = make_ident(fw)
    ident, d_id = p.ident
    ones = fw.sb("ones", [128, 128], F32)
    d_ones = Dep("ones")
    fw.op("pool", lambda e: e.memset(ones[:], 1.0), writes=[d_ones])
    p.ones = (ones, d_ones)
    identb = fw.sb("identb", [128, 128], BF16)
    d_idb = Dep("identb")
    cp(fw, "dve", identb[:], ident[:], [d_id], writes=[d_idb])
    p.identb = (identb, d_idb)
    Jf = fw.sb("Jf", [128, 128], F32)
    d_Jf = Dep("Jf")
    fw.op("pool", lambda e: e.memset(Jf[:], 0.0), writes=[d_Jf])
    fw.op("pool", lambda e: e.affine_select(out=Jf[:], in_=Jf[:], pattern=[[1, 128]], compare_op=ALU.not_equal, fill=1.0,
                                            base=-127, channel_multiplier=1), reads=[d_Jf], writes=[d_Jf])
    Jb = fw.sb("Jb", [128, 128], BF16)
    d_Jb = Dep("Jb")
    cp(fw, "dve", Jb[:], Jf[:], [d_Jf], writes=[d_Jb])
    p.Jb = (Jb, d_Jb)
    p.pj = Ring(fw, "pj", 2, [128, 512], F32, psum=True)
    p.tp = Ring(fw, "tp", 1, [128, 512], F32, psum=True)
    p.tpb_ps = Ring(fw, "tpb", 1, [128, 512], BF16, psum=True)
    p.OP = [(fw.ps("OP%d" % i, [128, 512], F32), Dep("OP%d" % i)) for i in range(4)]
    pjp = list(zip(p.pj.t, p.pj.d))
    p.pjbig = RingOf(pjp + p.OP)
    p.pjB = RingOf(pjp + list(zip(p.tp.t, p.tp.d)) + p.OP)
    for l in range(2):
        last = (l == 1)
        do_ctx = not last
        for ph, fn in enumerate((lambda: emit_A(p, l), lambda: emit_F(p, l, do_ctx), lambda: emit_B(p, l, do_ctx),
                                 lambda: emit_T(p, l, do_ctx), lambda: emit_C1(p, l, do_ctx), lambda: emit_C2(p, l, do_ctx, last))):
            if l * 6 + ph < KSTOP:
                fn()
    return nc


def rope_tables(S):
    rows = S // 64
    r = np.arange(rows, dtype=np.float32)
    cidx = np.arange(64, dtype=np.float32)
    row = np.broadcast_to(r[:, None], (rows, 64)).reshape(-1)
    col = np.broadcast_to(cidx[None, :], (rows, 64)).reshape(-1)
    inv = (1.0 / (np.float32(10000.0) ** (np.arange(16, dtype=np.float32) * np.float32(2.0) / np.float32(32)))).astype(np.float32)
    ang = np.stack([row[:, None] * inv, col[:, None] * inv], axis=1).astype(np.float32)
    return np.cos(ang).astype(np.float32).reshape(S, 32), np.sin(ang).astype(np.float32).reshape(S, 32)


def pos_table(l):
    f32 = np.float32
    t01 = np.linspace(0.0, 1.0, l, dtype=f32)
    tr = np.arange(l, dtype=f32)
    bands = np.linspace(1e-4, 15, 16, dtype=f32)
    ang = (f32(2.0 * math.pi) * tr[:, None] * bands[None, :] / f32(l)).astype(f32)
    z = np.concatenate([t01[:, None], np.cos(ang), np.sin(ang)], axis=-1).astype(f32)
    zt = np.ascontiguousarray(z.T)
    return np.ascontiguousarray(np.stack([zt, zt[:, ::-1]], axis=0))


def invc_table(L):
    nb = L // 128
    t = np.arange(L)
    out = np.zeros((128, 4 * nb), np.float32)
    for g, w in enumerate(POOL_WIN):
        cnt = (np.minimum(t + w // 2, L) - np.maximum(t - w // 2, 0)).astype(np.float32)
        out[:, g * nb:(g + 1) * nb] = (1.0 / cnt).reshape(nb, 128).T
    return out


_NC_CACHE = {}


def kernel(**inp):
    inp = {k: np.asarray(v) for k, v in inp.items()}
    x = inp["x"]
    B, S, _ = x.shape
    if S not in _NC_CACHE:
        _NC_CACHE[S] = build_all(S)
    nc = _NC_CACHE[S]
    cos, sin = rope_tables(S)
    cs = np.zeros((S + CTX, 64), np.float32)
    cs[:S, 0:32] = cos
    cs[:S, 32:64] = sin
    cs[S:, 0:32] = 1.0
    max_decay = math.log(1e-2) / 0.3
    min_decay = math.log(1e-2) / 1.5
    deltas = np.abs(np.linspace(min_decay, max_decay, 512, dtype=np.float32))
    negdelta = np.ascontiguousarray((-deltas).reshape(4, 128).T).astype(np.float32)
    fpool = np.zeros((4, 768), np.float32)
    for g, w in enumerate(POOL_WIN):
        fpool[g, 256 - (w // 2 - 1):256 + w // 2 + 1] = 1.0
    common = {"cs": cs, "zt_lat": pos_table(S), "zt_ctx": pos_table(CTX), "negdelta": negdelta,
              "invc_lat": invc_table(S), "invc_ctx": invc_table(CTX), "fpool": fpool.astype(NPBF),
              "final_g": np.ascontiguousarray(inp["final_g"][None, :])}
    for l in range(2):
        lw = {
            "w_in": inp["w_in"][l], "w_mod": inp["w_mod"][l], "b_mod": inp["b_mod"][l][None, :], "g1": inp["norm1_g"][l][None, :],
            "g2": inp["norm2_g"][l][None, :], "lamqk": inp["lam_qk"][l].reshape(1, 256), "subg": inp["subln_g"][l][None, :],
            "w_ao": inp["w_attn_o"][l], "w_pool": inp["w_pool"][l].reshape(512, 128), "psc": inp["pool_scale"][l].reshape(4, 128).T,
            "w_po": inp["w_pool_o"][l], "w_short": inp["w_short"][l], "b_short": inp["b_short"][l][None, :],
            "hf_w1": inp["hf_w1"][l], "hf_w2": inp["hf_w2"][l], "hf_w3": inp["hf_w3"][l],
            "hf_par": np.stack([inp["hf_b1"][l], inp["hf_freq1"][l], inp["hf_b2"][l], inp["hf_freq2"][l]], axis=1),
            "hyb": inp["hy_bias"][l].reshape(2, 4, 128).transpose(2, 0, 1).reshape(128, 8),
            "w_hy": inp["w_hy_o"][l], "w_out": inp["w_out"][l], "w_fi": inp["w_ffn_in"][l], "w_fo": inp["w_ffn_out"][l],
        }
        for k, v in lw.items():
            common["%s_%d" % (k, l)] = np.ascontiguousarray(v, dtype=np.float32)
    in_maps = []
    for b in range(B):
        m = dict(common)
        m["x"] = np.ascontiguousarray(x[b])
        m["ctxin"] = np.ascontiguousarray(inp["ctx"][b])
        m["cvec"] = np.ascontiguousarray(np.stack([inp["c"][b], inp["c_ctx"]], axis=0))
        in_maps.append(m)
    res = run_bass_kernel_spmd(nc, in_maps, core_ids=list(range(B)))
    return np.stack([r["out"] for r in res.results], axis=0).astype(np.float32)
```
